# Optimizing a Trainium2 kernel written in Bass

```python
import jax, jax.numpy as jnp
from jax import lax
import numpy as np

D_MODEL = 2048
BATCH = 8
SEQ = 4096
DEPTH = 4

CHUNK = 64
N_A_LAYERS = DEPTH // 2
N_B_LAYERS = DEPTH - N_A_LAYERS
GMLP_BLOCK = 128
GMLP_HALF = 3 * D_MODEL
GMLP_GROUPS = 8
GMLP_GROUP_DIM = GMLP_HALF // GMLP_GROUPS
N_HEADS = 16
HEAD_DIM = D_MODEL // N_HEADS
Q_BLOCK = 128
D_FF = 4 * D_MODEL
N_MOD = 6
EPS = 1e-6

kernel_name = "yoco_gmlp_forgetting_attn_adaln_trunk"


def rms_norm(x, g):
    xf = x.astype(jnp.float32)
    y = xf * lax.rsqrt(jnp.mean(xf * xf, axis=-1, keepdims=True) + EPS)
    return y.astype(x.dtype) * g


def layer_norm(x, g, b):
    xf = x.astype(jnp.float32)
    mu = jnp.mean(xf, axis=-1, keepdims=True)
    var = jnp.mean(jnp.square(xf - mu), axis=-1, keepdims=True)
    y = (xf - mu) * lax.rsqrt(var + EPS)
    return y.astype(x.dtype) * g + b


def modulate(h, shift, scale):
    return h * (1.0 + scale[:, None, :]) + shift[:, None, :]


def split_heads(t):
    b, s, _ = t.shape
    return t.reshape(b, s, N_HEADS, HEAD_DIM)


def gmlp_block_mask():
    idx = np.arange(GMLP_BLOCK) // CHUNK
    return jnp.asarray(idx[None, :] <= idx[:, None])


def gmlp_mixer(h, w_in, ln_g, ln_b, ws, bs, w_out):
    b, s, _ = h.shape
    z = jax.nn.gelu(h @ w_in, approximate=False)
    u, v = jnp.split(z, 2, axis=-1)
    v = layer_norm(v, ln_g, ln_b)
    n_blk = s // GMLP_BLOCK
    v = v.reshape(b, n_blk, GMLP_BLOCK, GMLP_GROUPS, GMLP_GROUP_DIM)
    w = jnp.where(gmlp_block_mask()[None], ws, jnp.zeros((), ws.dtype))
    sv = jnp.einsum('gij,bnjgc->bnigc', w, v) + bs.T[:, :, None]
    sv = sv.reshape(b, s, GMLP_HALF)
    return (u * sv) @ w_out


def squared_relu_mlp(h, w1, w2):
    return jnp.square(jax.nn.relu(h @ w1)) @ w2


def shared_kv(x, sc, kv_norm_g, kv_ada_w, kv_ada_b, w_kv, k_norm_g, w_f, b_f):
    shift, scale = jnp.split(sc @ kv_ada_w + kv_ada_b, 2, axis=-1)
    h = modulate(rms_norm(x, kv_norm_g), shift, scale)
    k, v = jnp.split(h @ w_kv, 2, axis=-1)
    k = rms_norm(split_heads(k), k_norm_g).transpose(0, 2, 1, 3)
    v = split_heads(v).transpose(0, 2, 1, 3)
    logf = jax.nn.log_sigmoid((h @ w_f).astype(jnp.float32) + b_f.astype(jnp.float32))
    fcum = jnp.cumsum(logf, axis=1).transpose(0, 2, 1)
    return k, v, fcum


def forgetting_attention(q, k, v, fcum):
    b, nh, s, dh = q.shape
    n_blk = s // Q_BLOCK
    qb = q.reshape(b, nh, n_blk, Q_BLOCK, dh).transpose(2, 0, 1, 3, 4)
    fq = fcum.reshape(b, nh, n_blk, Q_BLOCK).transpose(2, 0, 1, 3)
    key_pos = jnp.arange(s)
    inv_sqrt = 1.0 / float(np.sqrt(dh))

    def one_block(args):
        i, q_i, f_i = args
        logits = jnp.einsum('bhqd,bhkd->bhqk', q_i, k,
                            preferred_element_type=jnp.float32) * inv_sqrt
        logits = logits + (f_i[..., :, None] - fcum[..., None, :])
        q_pos = i * Q_BLOCK + jnp.arange(Q_BLOCK)
        logits = jnp.where(key_pos[None, :] <= q_pos[:, None], logits, -jnp.inf)
        p = jax.nn.softmax(logits, axis=-1)
        return jnp.einsum('bhqk,bhkd->bhqd', p.astype(v.dtype), v)

    out = lax.map(one_block, (jnp.arange(n_blk), qb, fq))
    return out.transpose(1, 2, 0, 3, 4).reshape(b, nh, s, dh)


def attention_mixer(h, k, v, fcum, wq, q_norm_g, wo):
    b, s, _ = h.shape
    q = rms_norm(split_heads(h @ wq), q_norm_g).transpose(0, 2, 1, 3)
    o = forgetting_attention(q, k, v, fcum)
    return o.transpose(0, 2, 1, 3).reshape(b, s, D_MODEL) @ wo


def setup_inputs(seed: int = 0) -> dict:
    key = jax.random.key(seed)
    ks = jax.random.split(key, 24)
    f32 = jnp.float32
    nrm = lambda k, shape, s: jax.random.normal(k, shape, f32) * s
    d = D_MODEL
    return {
        "x": nrm(ks[0], (BATCH, SEQ, d), 1.0),
        "c": nrm(ks[1], (BATCH, d), 1.0),
        "ada_w": nrm(ks[2], (DEPTH, d, N_MOD * d), d ** -0.5),
        "ada_b": nrm(ks[3], (DEPTH, N_MOD * d), 0.02),
        "norm_g": 1.0 + nrm(ks[4], (DEPTH, 2, d), 0.02),
        "mlp_w1": nrm(ks[5], (DEPTH, d, D_FF), d ** -0.5),
        "mlp_w2": nrm(ks[6], (DEPTH, D_FF, d), D_FF ** -0.5),
        "gmlp_w_in": nrm(ks[7], (N_A_LAYERS, d, 2 * GMLP_HALF), d ** -0.5),
        "gmlp_ln_g": 1.0 + nrm(ks[8], (N_A_LAYERS, GMLP_HALF), 0.02),
        "gmlp_ln_b": nrm(ks[9], (N_A_LAYERS, GMLP_HALF), 0.02),
        "gmlp_ws": nrm(ks[10], (N_A_LAYERS, GMLP_GROUPS, GMLP_BLOCK, GMLP_BLOCK), GMLP_BLOCK ** -0.5),
        "gmlp_bs": 1.0 + nrm(ks[11], (N_A_LAYERS, GMLP_GROUPS, GMLP_BLOCK), 0.1),
        "gmlp_w_out": nrm(ks[12], (N_A_LAYERS, GMLP_HALF, d), GMLP_HALF ** -0.5),
        "kv_norm_g": 1.0 + nrm(ks[13], (d,), 0.02),
        "kv_ada_w": nrm(ks[14], (d, 2 * d), d ** -0.5),
        "kv_ada_b": nrm(ks[15], (2 * d,), 0.02),
        "w_kv": nrm(ks[16], (d, 2 * d), d ** -0.5),
        "k_norm_g": 1.0 + nrm(ks[17], (HEAD_DIM,), 0.02),
        "w_f": nrm(ks[18], (d, N_HEADS), 0.1 * d ** -0.5),
        "b_f": jax.random.uniform(ks[19], (N_HEADS,), f32, 0.5, 5.0),
        "attn_wq": nrm(ks[20], (N_B_LAYERS, d, d), d ** -0.5),
        "q_norm_g": 1.0 + nrm(ks[21], (N_B_LAYERS, HEAD_DIM), 0.02),
        "attn_wo": nrm(ks[22], (N_B_LAYERS, d, d), d ** -0.5),
    }


def reference(x, c, ada_w, ada_b, norm_g, mlp_w1, mlp_w2, gmlp_w_in, gmlp_ln_g,
              gmlp_ln_b, gmlp_ws, gmlp_bs, gmlp_w_out, kv_norm_g, kv_ada_w, kv_ada_b,
              w_kv, k_norm_g, w_f, b_f, attn_wq, q_norm_g, attn_wo):
    sc = jax.nn.silu(c)
    k = v = fcum = None
    for layer in range(DEPTH):
        mod = sc @ ada_w[layer] + ada_b[layer]
        sh1, sc1, g1, sh2, sc2, g2 = jnp.split(mod, N_MOD, axis=-1)
        h = modulate(rms_norm(x, norm_g[layer, 0]), sh1, sc1)
        if layer < N_A_LAYERS:
            a = layer
            y = gmlp_mixer(h, gmlp_w_in[a], gmlp_ln_g[a], gmlp_ln_b[a],
                           gmlp_ws[a], gmlp_bs[a], gmlp_w_out[a])
        else:
            if layer == N_A_LAYERS:
                k, v, fcum = shared_kv(x, sc, kv_norm_g, kv_ada_w, kv_ada_b,
                                       w_kv, k_norm_g, w_f, b_f)
            bl = layer - N_A_LAYERS
            y = attention_mixer(h, k, v, fcum, attn_wq[bl], q_norm_g[bl], attn_wo[bl])
        x = x + g1[:, None, :] * y
        h = modulate(rms_norm(x, norm_g[layer, 1]), sh2, sc2)
        x = x + g2[:, None, :] * squared_relu_mlp(h, mlp_w1[layer], mlp_w2[layer])
    return x
```

```python
import numpy as np
import concourse.bass as bass
import concourse.mybir as mybir
from concourse.bass_utils import run_bass_kernel_spmd

F32 = mybir.dt.float32
BF16 = mybir.dt.bfloat16
AF = mybir.ActivationFunctionType
ALU = mybir.AluOpType

D = 2048
KC = 16
T = 512
SEQ = 4096
NT_FULL = SEQ // T
H = 16
EPS = 1e-6
NW = 3
NTMP = 4
INV_SQRT = 1.0 / float(np.sqrt(128.0))

CFG = {"NT": NT_FULL, "LAYERS": 4}

ENGS = ("pe", "act", "dve", "pool", "sp")


class Op(object):
    __slots__ = ("eng", "fn", "waits", "flag", "seq", "val", "sem", "isdma", "tag")


class Prog(object):
    def __init__(self, collect=False, unit_specs=None):
        self.collect = collect
        self.ops = {e: [] for e in ENGS}
        self.lastw = {}
        self.readers = {}
        self.known = {e: {} for e in ENGS}
        self.dmacnt = {}
        self.unit_specs = unit_specs if unit_specs is not None else []
        self.unit_next = 0
        self.unit_issued = 0
        self.rb = 0
        self.tmpc = 0
        self.cnt = {}
        self.tag = ""
        self.cur_tile = -1
        self.tile_units = {}

    def rot(self, name, n):
        v = self.cnt.get(name, 0)
        self.cnt[name] = (v + 1) % n
        return v

    def rbank(self):
        return self.rot("rb", 4)

    def tmpi(self):
        return self.rot("tmp", NTMP)

    def emit(self, eng, fn, reads=(), writes=(), dma_sem=None):
        op = Op()
        op.eng = eng
        op.fn = fn
        op.tag = self.tag
        op.isdma = dma_sem is not None
        op.flag = op.isdma
        deps = []
        lastw = self.lastw
        readers = self.readers
        for k in reads:
            w = lastw.get(k)
            if w is not None:
                deps.append(w)
        for k in writes:
            w = lastw.get(k)
            if w is not None:
                deps.append(w)
            r = readers.get(k)
            if r:
                deps.extend(r.values())
        kn = self.known[eng]
        waits = {}
        for d in deps:
            if d.isdma:
                s = d.sem
                v = d.val
            else:
                if d.eng == eng and eng == "pe":
                    continue
                s = d.eng
                v = d.seq
            if kn.get(s, -1) >= v:
                continue
            cur = waits.get(s)
            if cur is None or cur[0] < v:
                waits[s] = (v, d)
        wl = []
        for s, (v, d) in waits.items():
            kn[s] = v
            d.flag = True
            wl.append(d)
        op.waits = wl
        op.seq = len(self.ops[eng])
        self.ops[eng].append(op)
        if op.isdma:
            c = self.dmacnt.get(dma_sem, 0) + 1
            self.dmacnt[dma_sem] = c
            op.sem = dma_sem
            op.val = 16 * c
        else:
            op.sem = eng
            op.val = None
        for k in writes:
            lastw[k] = op
            readers[k] = {}
        for k in reads:
            if lastw.get(k) is op:
                continue
            r = readers.get(k)
            if r is None:
                r = {}
                readers[k] = r
            r[op.sem] = op
        return op

    def finalize(self):
        for e in ENGS:
            c = 0
            for op in self.ops[e]:
                if op.isdma:
                    continue
                if op.flag:
                    c += 1
                op.val = c
        tot = self.dmacnt.get("par", 0) * 16
        for e in ENGS:
            for op in self.ops[e]:
                if op.isdma and op.sem == "par":
                    op.val = tot


def build_program(nt, nlayers):
    nc = bass.Bass("TRN2", target_bir_lowering=False)
    dr = {}

    def din(name, shape):
        dr[name] = nc.dram_tensor(name, list(shape), F32, kind="ExternalInput").ap()
        return dr[name]

    x_d = din("x", (SEQ, D))
    c_d = din("c", (16, 128))
    ada_w = din("ada_w", (4, D, 6 * D))
    ada_b = din("ada_b", (384, 128))
    norm_g = din("norm_g", (128, 128))
    mlp_w1 = din("mlp_w1", (4, D, 4 * D))
    mlp_w2 = din("mlp_w2", (4, 4 * D, D))
    g_w_in = din("gmlp_w_in", (2, D, 6 * D))
    g_ln_g = din("gmlp_ln_g", (2, 48, 128))
    g_ln_b = din("gmlp_ln_b", (2, 48, 128))
    g_ws = din("gmlp_ws", (2, 8, 128, 128))
    g_bs = din("gmlp_bs", (2 * 8 * 128,))
    g_w_out = din("gmlp_w_out", (2, 3 * D, D))
    kv_norm_g = din("kv_norm_g", (16, 128))
    kv_ada_w = din("kv_ada_w", (D, 2 * D))
    kv_ada_b = din("kv_ada_b", (32, 128))
    w_kv = din("w_kv", (D, 2 * D))
    k_norm_g = din("k_norm_g", (1, 128))
    w_f = din("w_f", (D, H))
    b_f = din("b_f", (H,))
    attn_wq = din("attn_wq", (2, D, D))
    q_norm_g = din("q_norm_g", (2, 128))
    attn_wo = din("attn_wo", (2, D, D))
    out_d = nc.dram_tensor("out", [SEQ, D], F32, kind="ExternalOutput").ap()
    kt_scr = nc.dram_tensor("kt_scr", [H, NT_FULL, 128, T], BF16, kind="Internal").ap()
    v_scr = nc.dram_tensor("v_scr", [NT_FULL, H, 128, 4, 128], BF16, kind="Internal").ap()
    UPT = 224
    wcache_a = nc.dram_tensor("wcache_a", [UPT // 2, 128, KC * 512], BF16, kind="Internal").ap()
    wcache_b = nc.dram_tensor("wcache_b", [UPT // 2, 128, KC * 512], BF16, kind="Internal").ap()

    def wcache_at(j):
        return wcache_a[j] if j < UPT // 2 else wcache_b[j - UPT // 2]

    from contextlib import ExitStack
    es = ExitStack()
    with es:
        def sb(name, shape, dt):
            return es.enter_context(nc.sbuf_tensor(name, list(shape), dt))

        XT = sb("XT", (128, KC, T), F32)
        HT = sb("HT", (128, KC, T), BF16)
        HID = sb("HID", (128, 16, T), BF16)
        RA = sb("RA", (128, 48 * 512), BF16)
        WB = sb("WB", (128, NW, KC, 512), BF16)
        TMP = sb("TMP", (128, NTMP, 512), F32)
        SQ = sb("SQ", (128, 5, 512), BF16)
        RSTD = sb("RSTD", (128, 512), F32)
        IDT = sb("IDT", (128, 128), F32)
        TRI = sb("TRI", (128, 128), F32)
        SEL127 = sb("SEL127", (128, 128), F32)
        SEL0 = sb("SEL0", (128, 128), F32)
        ONESB = sb("ONESB", (128, 128), BF16)
        NEGM = sb("NEGM", (128, 128), BF16)
        IDB = sb("IDB", (128, 128), BF16)
        EPSC = sb("EPSC", (128, 1), F32)
        ONEC = sb("ONEC", (128, 1), F32)
        STG = sb("STG", (128, 2, 128), F32)
        BIAST = sb("BIAST", (128, 416), F32)
        MOD = sb("MOD", (128, 416), F32)
        ACOL = sb("ACOL", (128, 9, 16), F32)
        NG = sb("NG", (128, 128), F32)
        MISC = sb("MISC", (128, 32), F32)
        LNP = sb("LNP", (128, 2, 96), F32)
        SCT = sb("SCT", (128, 16), BF16)
        WST = sb("WST", (128, 2, 8, 128), BF16)
        RSB = sb("RSB", (128, 2, 8, 128), F32)
        BSB = sb("BSB", (128, 8, 128), F32)
        BFB = sb("BFB", (128, 16), F32)
        WF = sb("WF", (128, KC, H), BF16)
        NF = sb("NF", (128, 32, H), F32)
        CB = sb("CB", (128, 32, H), F32)
        STATS = sb("STATS", (128, 4, 72), F32)
        MV = sb("MV", (128, 4, 2), F32)
        LNS = sb("LNS", (128, 4, 4), F32)
        BT = sb("BT", (128, 2, 128), F32)
        ZS = sb("ZS", (128, 3, 16), F32)
        B4 = sb("B4", (128, 8, 4), F32)
        FACD = sb("FACD", (128, 4, H), F32)
        FAC = sb("FAC", (128, 4, H), F32)
        PS = es.enter_context(nc.psum_tensor("PS", [128, 8, 512], F32))

        RAf = RA[:, 0:16384].bitcast(F32)

        def xst(b, c0, c1):
            return RAf[:, b * 2048 + c0: b * 2048 + c1]

        def rblk(i):
            return RA[:, i * 512:(i + 1) * 512]

        def vtm(b, c0, c1):
            return RA[:, b * 6144 + c0: b * 6144 + c1]

        def qT(h):
            return rblk(h)

        def vst(b, c0, c1):
            return RA[:, 16 * 512 + b * 2048 + c0: 16 * 512 + b * 2048 + c1]

        def ktb(s):
            return rblk(32 + s)

        def vsb(s):
            return rblk(36 + s)

        def ptb(s):
            return rblk(40 + s)

        def ksg(s):
            return rblk(44 + s)

        def Rk(i):
            return ("R", i)

        def build(P):
            emit = P.emit

            def issue_unit(i):
                src, tile_, j = P.unit_specs[i]
                slot = i % NW
                ctile = j % 2
                if tile_ > ctile:
                    emit("pool", lambda e, s=slot, j=j: e.dma_start(out=WB[:, s].rearrange("p k c -> p (k c)"), in_=wcache_at(j)),
                         reads=[("wc", j)], writes=[("w", slot)], dma_sem="w%d" % slot)
                    return
                emit("pool", lambda e, s=slot, a=src: e.dma_start(out=WB[:, s], in_=a),
                     reads=(), writes=[("w", slot)], dma_sem="w%d" % slot)
                if tile_ == ctile and tile_ < nt - 1:
                    emit("sp", lambda e, s=slot, j=j: e.dma_start(out=wcache_at(j), in_=WB[:, s].rearrange("p k c -> p (k c)")),
                         reads=[("w", slot)], writes=[("wc", j)], dma_sem="wst%d" % slot)

            def unit(src2d, r0, c0):
                src = src2d[r0:r0 + 2048, c0:c0 + 512].rearrange("(j p) c -> p j c", p=128)
                if P.collect:
                    tl = P.cur_tile
                    j = P.tile_units.get(tl, 0)
                    P.tile_units[tl] = j + 1
                    P.unit_specs.append((src, tl, j))
                    return 0
                i = P.unit_next
                P.unit_next += 1
                lim = min(len(P.unit_specs), i + NW)
                while P.unit_issued < lim:
                    issue_unit(P.unit_issued)
                    P.unit_issued += 1
                return i % NW

            def mm(out, lhsT, rhs, start, stop, reads, writes):
                emit("pe", lambda e: e.matmul(out, lhsT, rhs, start=start, stop=stop, skip_group_check=True),
                     reads=reads, writes=writes)

            def tr(out, in_, ident, reads, writes):
                emit("pe", lambda e: e.transpose(out, in_, ident), reads=reads, writes=writes)

            def act(out, in_, func, reads, writes, bias=None, scale=None):
                kw = {}
                if bias is not None:
                    kw["bias"] = bias
                if scale is not None:
                    kw["scale"] = scale
                emit("act", lambda e: e.activation(out=out, in_=in_, func=func, **kw), reads=reads, writes=writes)

            def stt(out, in0, scalar, in1, op0, op1, reads, writes):
                emit("dve", lambda e: e.scalar_tensor_tensor(out=out, in0=in0, scalar=scalar, in1=in1, op0=op0, op1=op1),
                     reads=reads, writes=writes)

            def tt_(out, in0, in1, op, reads, writes, eng="dve"):
                emit(eng, lambda e: e.tensor_tensor(out=out, in0=in0, in1=in1, op=op), reads=reads, writes=writes)

            def ts(out, in0, s1, s2, op0, op1, reads, writes, eng="dve"):
                emit(eng, lambda e: e.tensor_scalar(out=out, in0=in0, scalar1=s1, scalar2=s2, op0=op0, op1=op1),
                     reads=reads, writes=writes)

            def cp(out, in_, reads, writes, eng="dve"):
                emit(eng, lambda e: e.tensor_copy(out=out, in_=in_), reads=reads, writes=writes)

            def recip(out, in_, reads, writes):
                emit("dve", lambda e: e.reciprocal(out=out, in_=in_), reads=reads, writes=writes)

            def dma(eng, out, in_, reads, writes, sem):
                emit(eng, lambda e: e.dma_start(out=out, in_=in_), reads=reads, writes=writes, dma_sem=sem)

            emit("pool", lambda e: e.memset(IDT[:], 0.0), writes=[("idt",)])
            emit("pool", lambda e: e.affine_select(out=IDT[:], in_=IDT[:], pattern=[[-1, 128]], compare_op=ALU.not_equal,
                                                   fill=1.0, base=0, channel_multiplier=1),
                 reads=[("idt",)], writes=[("idt",)])
            emit("pool", lambda e: e.memset(TRI[:], 1.0), writes=[("tri",)])
            emit("pool", lambda e: e.memset(ONESB[:], 1.0), writes=[("onesb",)])
            emit("pool", lambda e: e.memset(EPSC[:], EPS), writes=[("epsc",)])
            emit("pool", lambda e: e.memset(ONEC[:], 1.0), writes=[("onec",)])
            emit("pool", lambda e: e.affine_select(out=TRI[:], in_=TRI[:], pattern=[[1, 128]], compare_op=ALU.is_ge,
                                                   fill=0.0, base=0, channel_multiplier=-1),
                 reads=[("tri",)], writes=[("tri",)])
            emit("pool", lambda e: e.memset(SEL127[:], 0.0), writes=[("sel127",)])
            emit("pool", lambda e: e.affine_select(out=SEL127[:], in_=SEL127[:], pattern=[[0, 128]], compare_op=ALU.not_equal,
                                                   fill=1.0, base=-127, channel_multiplier=1),
                 reads=[("sel127",)], writes=[("sel127",)])
            emit("pool", lambda e: e.memset(SEL0[:], 0.0), writes=[("sel0",)])
            emit("pool", lambda e: e.affine_select(out=SEL0[:], in_=SEL0[:], pattern=[[0, 128]], compare_op=ALU.not_equal,
                                                   fill=1.0, base=0, channel_multiplier=1),
                 reads=[("sel0",)], writes=[("sel0",)])
            ts(NEGM[:], TRI[:], -1.0, 30000.0, ALU.add, ALU.mult, [("tri",)], [("negm",)], eng="pool")
            cp(IDB[:], IDT[:], [("idt",)], [("idb",)], eng="pool")
            emit("pool", lambda e: e.memset(FAC[:], 1.0), writes=[("fac",)])
            emit("pool", lambda e: e.memset(FACD[:], 0.0), writes=[("facd", j) for j in range(4)])

            dma("sp", BFB[:], b_f.partition_broadcast(128), (), [("bfb",)], "par")
            dma("pool", WF[:], w_f.rearrange("(kc p) h -> p kc h", p=128), (), [("wf",)], "par")

            def rows_T(src, nrows, dst, dkey, func=None):
                s = P.rot("stg", 2)
                dma("sp", STG[0:nrows, s, :], src, (), [("stg", s)], "stg%d" % s)
                b = P.rbank()
                tr(PS[:, b, 0:nrows], STG[0:nrows, s, :], IDT[0:nrows, 0:nrows], [("stg", s), ("idt",)], [("ps", b)])
                if func is None:
                    cp(dst, PS[:, b, 0:nrows], [("ps", b)], [dkey])
                else:
                    act(dst, PS[:, b, 0:nrows], func, [("ps", b)], [dkey])

            rows_T(ada_b[0:128, :], 128, BIAST[:, 0:128], ("biast", 0))
            rows_T(ada_b[128:256, :], 128, BIAST[:, 128:256], ("biast", 1))
            rows_T(ada_b[256:384, :], 128, BIAST[:, 256:384], ("biast", 2))
            rows_T(kv_ada_b[:, :], 32, BIAST[:, 384:416], ("biast", 3))
            rows_T(norm_g[:, :], 128, NG[:, :], ("ng",))
            rows_T(kv_norm_g[:, :], 16, MISC[:, 0:16], ("misc", 0))
            rows_T(k_norm_g[:, :], 1, MISC[:, 16:17], ("misc", 1))
            rows_T(q_norm_g[:, :], 2, MISC[:, 17:19], ("misc", 2))
            for a in range(2):
                rows_T(g_ln_g[a], 48, LNP[:, a, 0:48], ("lnp", a, 0))
                rows_T(g_ln_b[a], 48, LNP[:, a, 48:96], ("lnp", a, 1))
            rows_T(c_d[:, :], 16, SCT[:, :], ("sct",), func=AF.Silu)

            for a in range(2):
                for g in range(8):
                    s = P.rot("stg", 2)
                    dma("sp", STG[:, s, :], g_ws[a, g], (), [("stg", s)], "stg%d" % s)
                    b = P.rbank()
                    tr(PS[:, b, 0:128], STG[:, s, :], IDT[:], [("stg", s), ("idt",)], [("ps", b)])
                    cp(WST[:, a, g, :], PS[:, b, 0:128], [("ps", b)], [("wst", a, g)])
                    emit("dve", lambda e, a=a, g=g: e.memset(WST[64:128, a, g, 0:64], 0.0),
                         reads=[("wst", a, g)], writes=[("wst", a, g)])
                    b2 = P.rbank()
                    mm(PS[:, b2, 0:128], ONESB[:], WST[:, a, g, :], True, True, [("onesb",), ("wst", a, g)], [("ps", b2)])
                    cp(RSB[:, a, g, :], PS[:, b2, 0:128], [("ps", b2)], [("rsb", a, g)])

            P.tag = "mod"
            MB = 7
            modsrc = []
            for l in range(4):
                for u in range(24):
                    modsrc.append((ada_w[l], u * 512, l * 96 + u * 4))
            for u in range(8):
                modsrc.append((kv_ada_w, u * 512, 384 + u * 4))
            first_mod = True
            for (src2d, c0, col0) in modsrc:
                slot = unit(src2d, 0, c0)
                for m in range(4):
                    col = col0 + m
                    for kc in range(KC):
                        mm(PS[:, MB, col:col + 1], WB[:, slot, kc, m * 128:(m + 1) * 128], SCT[:, kc:kc + 1],
                           kc == 0, kc == KC - 1, [("w", slot), ("sct",)], [("ps", MB)])
            tt_(MOD[:, :], PS[:, MB, 0:416], BIAST[:, :], ALU.add,
                [("ps", MB)] + [("biast", i) for i in range(4)], [("mod",)])
            for l in range(4):
                for sub in range(2):
                    sc0 = l * 96 + (1 + 3 * sub) * 16
                    stt(ACOL[:, l * 2 + sub, :], MOD[:, sc0:sc0 + 16], 1.0, NG[:, l * 32 + sub * 16: l * 32 + sub * 16 + 16],
                        ALU.add, ALU.mult, [("mod",), ("ng",)], [("acol", l * 2 + sub)])
            stt(ACOL[:, 8, :], MOD[:, 400:416], 1.0, MISC[:, 0:16], ALU.add, ALU.mult,
                [("mod",), ("misc", 0)], [("acol", 8)])

            def modcol(l, m, kc):
                base = 384 + m * 16 if l == 4 else l * 96 + m * 16
                return MOD[:, base + kc: base + kc + 1]

            def stats_begin():
                P.ssb = P.rbank()
                P.ssn = 0

            def stats_sq(kc, eng):
                s = P.rot("sq", 5)
                if eng == "act":
                    act(SQ[:, s, :], XT[:, kc, :], AF.Square, [("x", kc)], [("sq", s)])
                else:
                    tt_(SQ[:, s, :], XT[:, kc, :], XT[:, kc, :], ALU.mult, [("x", kc)], [("sq", s)], eng=eng)

                def acc():
                    n = P.ssn
                    P.ssn = n + 1
                    mm(PS[:, P.ssb, :], ONESB[:], SQ[:, s, :], n == 0, n == KC - 1, [("sq", s), ("onesb",)], [("ps", P.ssb)])
                return acc

            def stats_end():
                P.tag = "norm"
                t = P.tmpi()
                act(TMP[:, t, :], PS[:, P.ssb, :], AF.Ln, [("ps", P.ssb), ("epsc",)], [("tmp", t)], bias=EPSC[:], scale=1.0 / D)
                act(RSTD[:], TMP[:, t, :], AF.Exp, [("tmp", t)], [("rstd",)], scale=-0.5)

            def norm_mod(aidx, l, mshift):
                P.tag = "norm"
                for kc in range(KC):
                    t2 = P.tmpi()
                    tt_(TMP[:, t2, :], XT[:, kc, :], RSTD[:], ALU.mult, [("x", kc), ("rstd",)], [("tmp", t2)])
                    if kc % 3 != 2:
                        act(HT[:, kc, :], TMP[:, t2, :], AF.Identity, [("tmp", t2), ("acol", aidx), ("mod",)], [("h", kc)],
                            bias=modcol(l, mshift, kc), scale=ACOL[:, aidx, kc:kc + 1])
                    else:
                        ts(HT[:, kc, :], TMP[:, t2, :], ACOL[:, aidx, kc:kc + 1], modcol(l, mshift, kc), ALU.mult, ALU.add,
                           [("tmp", t2), ("acol", aidx), ("mod",)], [("h", kc)], eng="pool")

            def gemm2(src2d, r0, l, mgate, final=False):
                P.tag = "gemm2"
                if final:
                    stats_begin()
                deferred = []
                for dg in range(4):
                    slot = unit(src2d, r0, dg * 512)
                    for j in range(12):
                        for dc in range(4):
                            mm(PS[:, 4 + dc, :], WB[:, slot, j, dc * 128:(dc + 1) * 128], HID[:, j, :],
                               j == 0, False, [("w", slot), ("hid", j)], [("ps", 4 + dc)])
                    for f in deferred:
                        f()
                    deferred = []
                    for dc in range(4):
                        for j in range(12, 16):
                            mm(PS[:, 4 + dc, :], WB[:, slot, j, dc * 128:(dc + 1) * 128], HID[:, j, :],
                               False, j == 15, [("w", slot), ("hid", j)], [("ps", 4 + dc)])
                        kc = dg * 4 + dc
                        stt(XT[:, kc, :], PS[:, 4 + dc, :], modcol(l, mgate, kc), XT[:, kc, :], ALU.mult, ALU.add,
                            [("ps", 4 + dc), ("mod",), ("x", kc)], [("x", kc)])
                        if final:
                            deferred.append(stats_sq(kc, "act"))
                for f in deferred:
                    f()
                if final:
                    stats_end()

            def gemm1_fm(src2d, c0, evac):
                slot = unit(src2d, 0, c0)
                pending = None
                for m in range(4):
                    b = P.rbank()
                    for kc in range(KC):
                        mm(PS[:, b, :], WB[:, slot, kc, m * 128:(m + 1) * 128], HT[:, kc, :],
                           kc == 0, kc == KC - 1, [("w", slot), ("h", kc)], [("ps", b)])
                    if pending is not None:
                        pending()
                    pending = evac(m, b)
                if pending is not None:
                    pending()

            def gemm1_tm(src2d, c0, evac):
                slot = unit(src2d, 0, c0)
                for blk in range(4):
                    b = P.rbank()
                    for kc in range(KC):
                        mm(PS[:, b, :], HT[:, kc, blk * 128:(blk + 1) * 128], WB[:, slot, kc, :],
                           kc == 0, kc == KC - 1, [("w", slot), ("h", kc)], [("ps", b)])
                    evac(blk, b)

            def head_norm(b, gcol, gkey, out_ap, out_keys, after=None):
                s = P.rot("sq", 5)
                act(SQ[:, s, :], PS[:, b, :], AF.Square, [("ps", b)], [("sq", s)])

                def tail():
                    b2 = 4 + P.rot("hnb", 4)
                    mm(PS[:, b2, :], ONESB[:], SQ[:, s, :], True, True, [("sq", s), ("onesb",)], [("ps", b2)])
                    t = P.tmpi()
                    act(TMP[:, t, :], PS[:, b2, :], AF.Ln, [("ps", b2), ("epsc",)], [("tmp", t)], bias=EPSC[:], scale=1.0 / 128.0)
                    t2 = P.tmpi()
                    act(TMP[:, t2, :], TMP[:, t, :], AF.Exp, [("tmp", t)], [("tmp", t2)], scale=-0.5)
                    stt(out_ap, PS[:, b, :], gcol, TMP[:, t2, :], ALU.mult, ALU.mult,
                        [("ps", b), ("tmp", t2), gkey], out_keys)
                    if after is not None:
                        after()
                return tail

            def mlp(l):
                for q in range(4):
                    P.tag = "mlp1"
                    def ev(m, b, q=q):
                        t = P.tmpi()
                        act(TMP[:, t, :], PS[:, b, :], AF.Relu, [("ps", b)], [("tmp", t)])
                        stt(HID[:, m_base[0] + m, :], PS[:, b, :], 0.0, TMP[:, t, :], ALU.max, ALU.mult,
                            [("ps", b), ("tmp", t)], [("hid", m_base[0] + m)])
                    m_base = [0]
                    for u in range(4):
                        m_base[0] = u * 4
                        gemm1_fm(mlp_w1[l], (q * 4 + u) * 512, ev)
                    gemm2(mlp_w2[l], q * 2048, l, 5, final=(q == 3 and l < nlayers - 1))

            def gmlp(l):
                a = l
                dma("sp", BSB[:].rearrange("p g i -> p (g i)"), g_bs[a * 1024:(a + 1) * 1024].partition_broadcast(128),
                    (), [("bsb",)], "bsb")
                P.tag = "gmlp_v"
                for n in range(12):
                    def ev(blk, b, n=n):
                        act(vtm(blk, n * 512, (n + 1) * 512), PS[:, b, :], AF.Gelu, [("ps", b)], [Rk(blk * 12 + n)])
                        emit("dve", lambda e, blk=blk, n=n: e.bn_stats(out=STATS[:, blk, n * 6:(n + 1) * 6],
                                                                        in_=vtm(blk, n * 512, (n + 1) * 512)),
                             reads=[Rk(blk * 12 + n)], writes=[("stats", blk, n)])
                    gemm1_tm(g_w_in[a], 6144 + n * 512, ev)
                P.tag = "gmlp_ln"
                for blk in range(4):
                    emit("dve", lambda e, blk=blk: e.bn_aggr(out=MV[:, blk, :], in_=STATS[:, blk, :]),
                         reads=[("stats", blk, n) for n in range(12)], writes=[("mv", blk)])
                    act(LNS[:, blk, 0:1], MV[:, blk, 1:2], AF.Sqrt, [("mv", blk), ("epsc",)], [("lns", blk, 0)],
                        bias=EPSC[:], scale=1.0)
                    recip(LNS[:, blk, 1:2], LNS[:, blk, 0:1], [("lns", blk, 0)], [("lns", blk, 1)])
                    stt(LNS[:, blk, 2:3], MV[:, blk, 0:1], -1.0, LNS[:, blk, 1:2], ALU.mult, ALU.mult,
                        [("mv", blk), ("lns", blk, 1)], [("lns", blk, 2)])
                    for n in range(12):
                        ts(vtm(blk, n * 512, (n + 1) * 512), vtm(blk, n * 512, (n + 1) * 512),
                           LNS[:, blk, 1:2], LNS[:, blk, 2:3], ALU.mult, ALU.add,
                           [Rk(blk * 12 + n), ("lns", blk, 1), ("lns", blk, 2)], [Rk(blk * 12 + n)])
                for third in range(3):
                    P.tag = "gmlp_u"
                    for u in range(4):
                        def ev(m, b, third=third, u=u):
                            ccl = u * 4 + m
                            cc = third * 16 + ccl
                            g = cc // 6
                            tu = P.tmpi()
                            act(TMP[:, tu, :], PS[:, b, :], AF.Gelu, [("ps", b)], [("tmp", tu)])
                            sb_ = P.rbank()
                            for blk in range(4):
                                mm(PS[:, sb_, blk * 128:(blk + 1) * 128], vtm(blk, cc * 128, (cc + 1) * 128), WST[:, a, g, :],
                                   True, True, [Rk(blk * 12 + cc // 4), ("wst", a, g)], [("ps", sb_)])
                            bs_ = P.rot("bt", 2)
                            stt(BT[:, bs_, :], RSB[:, a, g, :], LNP[:, a, 48 + cc: 48 + cc + 1], BSB[:, g, :],
                                ALU.mult, ALU.add, [("rsb", a, g), ("lnp", a, 1), ("bsb",)], [("bt", bs_)])
                            tsv = P.tmpi()
                            for blk in range(4):
                                stt(TMP[:, tsv, blk * 128:(blk + 1) * 128], PS[:, sb_, blk * 128:(blk + 1) * 128],
                                    LNP[:, a, cc:cc + 1], BT[:, bs_, :], ALU.mult, ALU.add,
                                    [("ps", sb_), ("lnp", a, 0), ("bt", bs_)],
                                    [("tmp", tsv, blk)] + ([("tmp", tsv)] if blk == 0 else []))
                            tt_(HID[:, ccl, :], TMP[:, tsv, :], TMP[:, tu, :], ALU.mult,
                                [("tmp", tsv)] + [("tmp", tsv, blk) for blk in range(4)] + [("tmp", tu)], [("hid", ccl)])
                        gemm1_fm(g_w_in[a], (third * 4 + u) * 512, ev)
                    gemm2(g_w_out[a], third * 2048, l, 2, final=(third == 2))

            def kv_phase(tti):
                norm_mod(8, 4, 0)
                P.tag = "kv"
                for u in range(4):
                    def ev(m, b, u=u):
                        hd = u * 4 + m
                        s = P.rot("ksg", 2)
                        return head_norm(b, MISC[:, 16:17], ("misc", 1), ksg(s), [Rk(44 + s)],
                                         after=lambda: dma("sp", kt_scr[hd, tti], ksg(s), [Rk(44 + s)], [("ktd", hd, tti)],
                                                           "ksg%d" % s))
                    gemm1_fm(w_kv, u * 512, ev)
                for u in range(4):
                    def ev(blk, b, u=u):
                        if (u + blk) % 2 == 0:
                            cp(vst(blk, u * 512, (u + 1) * 512), PS[:, b, :], [("ps", b)], [Rk(16 + blk * 4 + u)])
                        else:
                            act(vst(blk, u * 512, (u + 1) * 512), PS[:, b, :], AF.Identity, [("ps", b)], [Rk(16 + blk * 4 + u)])
                    gemm1_tm(w_kv, 2048 + u * 512, ev)
                for blk in range(4):
                    dma("sp", v_scr[tti, :, :, blk, :].rearrange("h p d -> p h d"),
                        vst(blk, 0, 2048).rearrange("p (h d) -> p h d", h=H),
                        [Rk(16 + blk * 4 + u) for u in range(4)], [("vd", tti, blk)], "vst%d" % blk)
                for blk in range(4):
                    gb = tti * 4 + blk
                    fb = P.rbank()
                    for kc in range(KC):
                        mm(PS[:, fb, 0:H], HT[:, kc, blk * 128:(blk + 1) * 128], WF[:, kc, :],
                           kc == 0, kc == KC - 1, [("h", kc), ("wf",)], [("ps", fb)])
                    tt_(ZS[:, 0, :], PS[:, fb, 0:H], BFB[:], ALU.add, [("ps", fb), ("bfb",)], [("zs", 0)])
                    act(ZS[:, 1, :], ZS[:, 0, :], AF.Exp, [("zs", 0)], [("zs", 1)], scale=-1.0)
                    act(ZS[:, 2, :], ZS[:, 1, :], AF.Ln, [("zs", 1), ("onec",)], [("zs", 2)], bias=ONEC[:], scale=1.0)
                    cb_ = P.rbank()
                    mm(PS[:, cb_, 0:H], TRI[:], ZS[:, 2, :], True, gb == 0, [("tri",), ("zs", 2)], [("ps", cb_)])
                    if gb > 0:
                        mm(PS[:, cb_, 0:H], SEL127[:], NF[:, gb - 1, :], False, True,
                           [("sel127",), ("nf", gb - 1)], [("ps", cb_)])
                    cp(NF[:, gb, :], PS[:, cb_, 0:H], [("ps", cb_)], [("nf", gb)])
                    c2 = P.rbank()
                    mm(PS[:, c2, 0:H], SEL0[:], NF[:, gb, :], True, True, [("sel0",), ("nf", gb)], [("ps", c2)])
                    cp(CB[:, gb, :], PS[:, c2, 0:H], [("ps", c2)], [("cb", gb)])

            def attention(l, tti):
                bl = l - 2
                P.tag = "attn_q"
                for u in range(4):
                    def ev(m, b, u=u):
                        hd = u * 4 + m
                        return head_norm(b, MISC[:, 17 + bl:18 + bl], ("misc", 2), qT(hd), [Rk(hd)])
                    gemm1_fm(attn_wq[bl], u * 512, ev)
                P.tag = "attn"
                if tti > 0:
                    for j in range(1, 4):
                        tt_(FACD[:, j, :], CB[:, tti * 4 + j, :], CB[:, tti * 4, :], ALU.subtract,
                            [("cb", tti * 4 + j), ("cb", tti * 4)], [("facd", j)])
                    act(FAC[:, 1:4, :], FACD[:, 1:4, :], AF.Exp, [("facd", j) for j in range(1, 4)], [("fac",)], scale=-1.0)
                groups = [(hd, st) for hd in range(H) for st in range(tti + 1)]
                steps = [(g, i) for g in range(len(groups)) for i in range(4)]
                LA = 3
                gslots = {}
                pend = {}
                hstate = {}

                def load_group(g):
                    hd, st = groups[g]
                    ks = P.rot("kt", 4)
                    vs = P.rot("vs", 4)
                    dma("sp", ktb(ks), kt_scr[hd, st], [("ktd", hd, st)], [Rk(32 + ks)], "kt%d" % ks)
                    dma("sp", vsb(vs).rearrange("p (b d) -> p b d", b=4), v_scr[st, hd],
                        [("vd", st, blk) for blk in range(4)], [Rk(36 + vs)], "vs%d" % vs)
                    gslots[g] = (ks, vs)

                def s_stage(k):
                    g, i = steps[k]
                    hd, st = groups[g]
                    if i == 0 and g + 2 < len(groups):
                        load_group(g + 2)
                    ks, vs = gslots[g]
                    gsb = st * 4 + i
                    diag = (st == tti)
                    j0 = i if diag else 0
                    c0 = j0 * 128
                    sbk = P.rbank()
                    mm(PS[:, sbk, c0:512], ktb(ks)[:, i * 128:(i + 1) * 128], qT(hd)[:, c0:512], True, not diag,
                       [Rk(32 + ks), Rk(hd)], [("ps", sbk)])
                    if diag:
                        mm(PS[:, sbk, c0:c0 + 128], IDB[:], NEGM[:], False, True, [("idb",), ("negm",)], [("ps", sbk)])
                    pt = P.rot("pt", 4)
                    b4 = P.rot("b4", 8)
                    if diag:
                        ts(B4[:, b4, :], CB[:, tti * 4:tti * 4 + 4, hd], -1.0, NF[:, gsb, hd:hd + 1], ALU.mult, ALU.add,
                           [("cb", tti * 4 + j) for j in range(4)] + [("nf", gsb)], [("b4", b4)], eng="pool")
                        for j in range(j0, 4):
                            act(ptb(pt)[:, j * 128:(j + 1) * 128], PS[:, sbk, j * 128:(j + 1) * 128], AF.Exp,
                                [("ps", sbk), ("b4", b4)], [("pt", pt, j)] + ([Rk(40 + pt)] if j == j0 else []),
                                bias=B4[:, b4, j:j + 1], scale=INV_SQRT)
                    else:
                        ts(B4[:, b4, 0:1], NF[:, gsb, hd:hd + 1], CB[:, tti * 4, hd:hd + 1], None, ALU.subtract, ALU.bypass,
                           [("cb", tti * 4), ("nf", gsb)], [("b4", b4)], eng="pool")
                        act(ptb(pt)[:, :], PS[:, sbk, :], AF.Exp, [("ps", sbk), ("b4", b4)],
                            [("pt", pt, j) for j in range(4)] + [Rk(40 + pt)], bias=B4[:, b4, 0:1], scale=INV_SQRT)
                    pend[k] = (hd, st, i, vs, pt, c0, j0)

                def pv_stage(k):
                    hd, st, i, vs, pt, c0, j0 = pend.pop(k)
                    diag = (st == tti)
                    po = 6 if diag else 4
                    if tti == 0:
                        po = 4 + 2 * (hd % 2)
                    pd = po + 1
                    first = (i == 0) if diag else (st == 0 and i == 0)
                    last = (i == 3) if diag else (st == tti - 1 and i == 3)
                    rd = [("pt", pt, j) for j in range(j0, 4)] + [Rk(40 + pt)]
                    mm(PS[:, po, c0:512], vsb(vs)[:, i * 128:(i + 1) * 128], ptb(pt)[:, c0:512], first, last,
                       rd + [Rk(36 + vs)], [("ps", po)])
                    mm(PS[:, pd, c0:512], ONESB[:], ptb(pt)[:, c0:512], first, last,
                       rd + [("onesb",)], [("ps", pd)])
                    if last and not diag:
                        to = P.tmpi()
                        td = P.tmpi()
                        for j in range(4):
                            ts(TMP[:, to, j * 128:(j + 1) * 128], PS[:, 4, j * 128:(j + 1) * 128], FAC[:, j, hd:hd + 1], None,
                               ALU.mult, ALU.bypass, [("ps", 4), ("fac",)], [("tmp", to, j)] + ([("tmp", to)] if j == 0 else []))
                            ts(TMP[:, td, j * 128:(j + 1) * 128], PS[:, 5, j * 128:(j + 1) * 128], FAC[:, j, hd:hd + 1], None,
                               ALU.mult, ALU.bypass, [("ps", 5), ("fac",)], [("tmp", td, j)] + ([("tmp", td)] if j == 0 else []))
                        hstate[hd] = (to, td)
                    if last and diag:
                        if tti == 0:
                            td = P.tmpi()
                            recip(TMP[:, td, :], PS[:, pd, :], [("ps", pd)], [("tmp", td)])
                            tt_(HID[:, hd, :], PS[:, po, :], TMP[:, td, :], ALU.mult, [("ps", po), ("tmp", td)], [("hid", hd)])
                        else:
                            to, td = hstate.pop(hd)
                            allk_o = [("tmp", to)] + [("tmp", to, j) for j in range(4)]
                            allk_d = [("tmp", td)] + [("tmp", td, j) for j in range(4)]
                            tt_(TMP[:, td, :], PS[:, 7, :], TMP[:, td, :], ALU.add, [("ps", 7)] + allk_d, [("tmp", td)])
                            recip(TMP[:, td, :], TMP[:, td, :], [("tmp", td)], [("tmp", td)])
                            tt_(TMP[:, to, :], PS[:, 6, :], TMP[:, to, :], ALU.add, [("ps", 6)] + allk_o, [("tmp", to)])
                            tt_(HID[:, hd, :], TMP[:, to, :], TMP[:, td, :], ALU.mult, [("tmp", to), ("tmp", td)], [("hid", hd)])

                for g in range(min(2, len(groups))):
                    load_group(g)
                for k in range(len(steps) + LA):
                    if k < len(steps):
                        s_stage(k)
                    if k - LA >= 0:
                        pv_stage(k - LA)
                gemm2(attn_wo[bl], 0, l, 2, final=True)

            for tti in range(nt):
                P.tag = "xin"
                P.cur_tile = tti
                dma("sp", RAf[:, :].rearrange("p (b d) -> p b d", b=4),
                    x_d[tti * T:(tti + 1) * T, :].rearrange("(b p) d -> p b d", p=128),
                    (), [Rk(i) for i in range(32)], "xin")
                for kc in range(KC):
                    b = P.rbank()
                    for blk in range(4):
                        tr(PS[:, b, blk * 128:(blk + 1) * 128], xst(blk, kc * 128, (kc + 1) * 128), IDT[:],
                           [Rk(blk * 8 + kc // 2), ("idt",)], [("ps", b)])
                    if kc % 2 == 0:
                        cp(XT[:, kc, :], PS[:, b, :], [("ps", b)], [("x", kc)])
                    else:
                        act(XT[:, kc, :], PS[:, b, :], AF.Identity, [("ps", b)], [("x", kc)])
                P.tag = "norm"
                stats_begin()
                for kc in range(KC):
                    stats_sq(kc, "pool" if kc % 4 == 1 else ("dve" if kc % 8 == 7 else "act"))()
                stats_end()
                for l in range(nlayers):
                    if l == 2:
                        kv_phase(tti)
                    norm_mod(l * 2, l, 0)
                    if l < 2:
                        gmlp(l)
                    else:
                        attention(l, tti)
                    norm_mod(l * 2 + 1, l, 3)
                    mlp(l)
                P.tag = "xout"
                for blk in range(4):
                    for q in range(4):
                        b = P.rbank()
                        for r in range(4):
                            kc = q * 4 + r
                            tr(PS[:, b, r * 128:(r + 1) * 128], XT[:, kc, blk * 128:(blk + 1) * 128], IDT[:],
                               [("x", kc), ("idt",)], [("ps", b)])
                        if q % 2 == 0:
                            cp(xst(blk, q * 512, (q + 1) * 512), PS[:, b, :], [("ps", b)], [Rk(blk * 8 + q * 2), Rk(blk * 8 + q * 2 + 1)])
                        else:
                            act(xst(blk, q * 512, (q + 1) * 512), PS[:, b, :], AF.Identity, [("ps", b)],
                                [Rk(blk * 8 + q * 2), Rk(blk * 8 + q * 2 + 1)])
                    dma("sp", out_d[tti * T + blk * 128: tti * T + (blk + 1) * 128, :], xst(blk, 0, 2048),
                        [Rk(blk * 8 + i) for i in range(8)], [("outd", tti, blk)], "xo%d" % blk)

        P1 = Prog(collect=True)
        build(P1)
        P2 = Prog(collect=False, unit_specs=P1.unit_specs)
        build(P2)
        P2.finalize()
        P = P2
        nc._prog = P

        sem_names = list(ENGS[:4]) + sorted(P.dmacnt.keys())
        sems = {}
        for nm in sem_names:
            sems[nm] = es.enter_context(nc.semaphore("s_" + nm))
        block = es.enter_context(nc.Block())

        def replay(engname, e):
            for op in P.ops[engname]:
                for d in op.waits:
                    e.wait_ge(sems[d.sem], d.val)
                ins = op.fn(e)
                if op.isdma:
                    ins.then_inc(sems[op.sem], 16)
                elif op.flag:
                    ins.then_inc(sems[op.sem], 1)
            if engname == "sp":
                for nm, c in P.dmacnt.items():
                    if nm.startswith("xo"):
                        e.wait_ge(sems[nm], 16 * c)

        @block.tensor
        def _(e):
            replay("pe", e)

        @block.scalar
        def _(e):
            replay("act", e)

        @block.vector
        def _(e):
            replay("dve", e)

        @block.gpsimd
        def _(e):
            replay("pool", e)

        @block.sync
        def _(e):
            replay("sp", e)

    return nc


_CACHE = {}


def make_in_maps(inputs, nt):
    f = lambda a: np.ascontiguousarray(np.asarray(a, dtype=np.float32))
    shared = {
        "ada_w": f(inputs["ada_w"]),
        "ada_b": f(inputs["ada_b"]).reshape(384, 128),
        "norm_g": f(inputs["norm_g"]).reshape(128, 128),
        "mlp_w1": f(inputs["mlp_w1"]),
        "mlp_w2": f(inputs["mlp_w2"]),
        "gmlp_w_in": f(inputs["gmlp_w_in"]),
        "gmlp_ln_g": f(inputs["gmlp_ln_g"]).reshape(2, 48, 128),
        "gmlp_ln_b": f(inputs["gmlp_ln_b"]).reshape(2, 48, 128),
        "gmlp_ws": f(inputs["gmlp_ws"]),
        "gmlp_bs": f(inputs["gmlp_bs"]).reshape(2 * 8 * 128),
        "gmlp_w_out": f(inputs["gmlp_w_out"]),
        "kv_norm_g": f(inputs["kv_norm_g"]).reshape(16, 128),
        "kv_ada_w": f(inputs["kv_ada_w"]),
        "kv_ada_b": f(inputs["kv_ada_b"]).reshape(32, 128),
        "w_kv": f(inputs["w_kv"]),
        "k_norm_g": f(inputs["k_norm_g"]).reshape(1, 128),
        "w_f": f(inputs["w_f"]),
        "b_f": f(inputs["b_f"]).reshape(16),
        "attn_wq": f(inputs["attn_wq"]),
        "q_norm_g": f(inputs["q_norm_g"]).reshape(2, 128),
        "attn_wo": f(inputs["attn_wo"]),
    }
    x = f(inputs["x"])
    c = f(inputs["c"])
    maps = []
    for b in range(8):
        m = dict(shared)
        m["x"] = x[b]
        m["c"] = c[b].reshape(16, 128)
        maps.append(m)
    return maps


def kernel(**inputs):
    nt = CFG["NT"]
    nl = CFG["LAYERS"]
    key = (nt, nl)
    if key not in _CACHE:
        _CACHE[key] = build_program(nt, nl)
    nc = _CACHE[key]
    maps = make_in_maps(inputs, nt)
    res = run_bass_kernel_spmd(nc, maps, core_ids=list(range(8)))
    out = np.stack([np.asarray(r["out"], dtype=np.float32) for r in res.results], axis=0)
    return out
```

```python
import numpy as np
import concourse.bass as bass
import concourse.mybir as mybir
from concourse.bass_utils import run_bass_kernel_spmd

F32 = mybir.dt.float32
BF16 = mybir.dt.bfloat16
AF = mybir.ActivationFunctionType
ALU = mybir.AluOpType

D = 2048
KC = 16
T = 512
SEQ = 4096
NT_FULL = SEQ // T
H = 16
EPS = 1e-6
NW = 3
NTMP = 4
INV_SQRT = 1.0 / float(np.sqrt(128.0))

CFG = {"NT": NT_FULL, "LAYERS": 4}

ENGS = ("pe", "act", "dve", "pool", "sp")


class Op(object):
    __slots__ = ("eng", "fn", "waits", "flag", "seq", "val", "sem", "isdma", "tag")


class Prog(object):
    def __init__(self, collect=False, unit_specs=None):
        self.collect = collect
        self.ops = {e: [] for e in ENGS}
        self.lastw = {}
        self.readers = {}
        self.known = {e: {} for e in ENGS}
        self.dmacnt = {}
        self.unit_specs = unit_specs if unit_specs is not None else []
        self.unit_next = 0
        self.unit_issued = 0
        self.rb = 0
        self.tmpc = 0
        self.cnt = {}
        self.tag = ""
        self.cur_tile = -1
        self.tile_units = {}
        self.in_mod = False
        self.mod_next = 0
        self.ucnt = 0

    def rot(self, name, n):
        v = self.cnt.get(name, 0)
        self.cnt[name] = (v + 1) % n
        return v

    def rbank(self):
        return self.rot("rb", 4)

    def tmpi(self):
        return self.rot("tmp", NTMP)

    def emit(self, eng, fn, reads=(), writes=(), dma_sem=None):
        op = Op()
        op.eng = eng
        op.fn = fn
        op.tag = self.tag
        op.isdma = dma_sem is not None
        op.flag = op.isdma
        deps = []
        lastw = self.lastw
        readers = self.readers
        for k in reads:
            w = lastw.get(k)
            if w is not None:
                deps.append(w)
        for k in writes:
            w = lastw.get(k)
            if w is not None:
                deps.append(w)
            r = readers.get(k)
            if r:
                deps.extend(r.values())
        kn = self.known[eng]
        waits = {}
        for d in deps:
            if d.isdma:
                s = d.sem
                v = d.val
            else:
                if d.eng == eng and eng == "pe":
                    continue
                s = d.eng
                v = d.seq
            if kn.get(s, -1) >= v:
                continue
            cur = waits.get(s)
            if cur is None or cur[0] < v:
                waits[s] = (v, d)
        wl = []
        for s, (v, d) in waits.items():
            kn[s] = v
            d.flag = True
            wl.append(d)
        op.waits = wl
        op.seq = len(self.ops[eng])
        self.ops[eng].append(op)
        if op.isdma:
            c = self.dmacnt.get(dma_sem, 0) + 1
            self.dmacnt[dma_sem] = c
            op.sem = dma_sem
            op.val = 16 * c
        else:
            op.sem = eng
            op.val = None
        for k in writes:
            lastw[k] = op
            readers[k] = {}
        for k in reads:
            if lastw.get(k) is op:
                continue
            r = readers.get(k)
            if r is None:
                r = {}
                readers[k] = r
            r[op.sem] = op
        return op

    def finalize(self):
        for e in ENGS:
            c = 0
            for op in self.ops[e]:
                if op.isdma:
                    continue
                if op.flag:
                    c += 1
                op.val = c
        tot = self.dmacnt.get("par", 0) * 16
        for e in ENGS:
            for op in self.ops[e]:
                if op.isdma and op.sem == "par":
                    op.val = tot


def build_program(nt, nlayers):
    nc = bass.Bass("TRN2", target_bir_lowering=False)
    dr = {}

    def din(name, shape):
        dr[name] = nc.dram_tensor(name, list(shape), F32, kind="ExternalInput").ap()
        return dr[name]

    x_d = din("x", (SEQ, D))
    c_d = din("c", (16, 128))
    ada_w = din("ada_w", (4, D, 6 * D))
    ada_b = din("ada_b", (384, 128))
    norm_g = din("norm_g", (128, 128))
    mlp_w1 = din("mlp_w1", (4, D, 4 * D))
    mlp_w2 = din("mlp_w2", (4, 4 * D, D))
    g_w_in = din("gmlp_w_in", (2, D, 6 * D))
    g_ln_g = din("gmlp_ln_g", (2, 48, 128))
    g_ln_b = din("gmlp_ln_b", (2, 48, 128))
    g_ws = din("gmlp_ws", (2, 8, 128, 128))
    g_bs = din("gmlp_bs", (2 * 8 * 128,))
    g_w_out = din("gmlp_w_out", (2, 3 * D, D))
    kv_norm_g = din("kv_norm_g", (16, 128))
    kv_ada_w = din("kv_ada_w", (D, 2 * D))
    kv_ada_b = din("kv_ada_b", (32, 128))
    w_kv = din("w_kv", (D, 2 * D))
    k_norm_g = din("k_norm_g", (1, 128))
    w_f = din("w_f", (D, H))
    b_f = din("b_f", (H,))
    attn_wq = din("attn_wq", (2, D, D))
    q_norm_g = din("q_norm_g", (2, 128))
    attn_wo = din("attn_wo", (2, D, D))
    out_d = nc.dram_tensor("out", [SEQ, D], F32, kind="ExternalOutput").ap()
    kt_scr = nc.dram_tensor("kt_scr", [H, NT_FULL, 128, T], BF16, kind="Internal").ap()
    v_scr = nc.dram_tensor("v_scr", [NT_FULL, H, 128, 4, 128], BF16, kind="Internal").ap()
    UPT = 224
    wcache_a = nc.dram_tensor("wcache_a", [UPT // 2, 128, KC * 512], BF16, kind="Internal").ap()
    wcache_b = nc.dram_tensor("wcache_b", [UPT // 2, 128, KC * 512], BF16, kind="Internal").ap()

    def wcache_at(j):
        return wcache_a[j] if j < UPT // 2 else wcache_b[j - UPT // 2]

    from contextlib import ExitStack
    es = ExitStack()
    with es:
        def sb(name, shape, dt):
            return es.enter_context(nc.sbuf_tensor(name, list(shape), dt))

        XT = sb("XT", (128, KC, T), F32)
        HT = sb("HT", (128, KC, T), BF16)
        HID = sb("HID", (128, 16, T), BF16)
        RA = sb("RA", (128, 48 * 512), BF16)
        WB = sb("WB", (128, NW, KC, 512), BF16)
        TMP = sb("TMP", (128, NTMP, 512), F32)
        SQ = sb("SQ", (128, 5, 512), BF16)
        RSTD = sb("RSTD", (128, 512), F32)
        IDT = sb("IDT", (128, 128), F32)
        TRI = sb("TRI", (128, 128), F32)
        SEL127 = sb("SEL127", (128, 128), F32)
        SEL0 = sb("SEL0", (128, 128), F32)
        ONESB = sb("ONESB", (128, 128), BF16)
        NEGM = sb("NEGM", (128, 128), BF16)
        IDB = sb("IDB", (128, 128), BF16)
        EPSC = sb("EPSC", (128, 1), F32)
        ONEC = sb("ONEC", (128, 1), F32)
        STG = sb("STG", (128, 2, 128), F32)
        BIAST = sb("BIAST", (128, 416), F32)
        MOD = sb("MOD", (128, 416), F32)
        ACOL = sb("ACOL", (128, 9, 16), F32)
        NG = sb("NG", (128, 128), F32)
        MISC = sb("MISC", (128, 32), F32)
        LNP = sb("LNP", (128, 2, 96), F32)
        SCT = sb("SCT", (128, 16), BF16)
        WST = sb("WST", (128, 2, 8, 128), BF16)
        RSB = sb("RSB", (128, 2, 8, 128), F32)
        BSB = sb("BSB", (128, 8, 128), F32)
        BFB = sb("BFB", (128, 16), F32)
        WF = sb("WF", (128, KC, H), BF16)
        NF = sb("NF", (128, 32, H), F32)
        CB = sb("CB", (128, 32, H), F32)
        STATS = sb("STATS", (128, 4, 72), F32)
        MV = sb("MV", (128, 4, 2), F32)
        LNS = sb("LNS", (128, 4, 4), F32)
        BT = sb("BT", (128, 2, 128), F32)
        ZS = sb("ZS", (128, 3, 16), F32)
        B4 = sb("B4", (128, 8, 4), F32)
        FACD = sb("FACD", (128, 4, H), F32)
        FAC = sb("FAC", (128, 4, H), F32)
        PS = es.enter_context(nc.psum_tensor("PS", [128, 8, 512], F32))

        RAf = RA[:, 0:16384].bitcast(F32)

        def xst(b, c0, c1):
            return RAf[:, b * 2048 + c0: b * 2048 + c1]

        def rblk(i):
            return RA[:, i * 512:(i + 1) * 512]

        def vtm(b, c0, c1):
            return RA[:, b * 6144 + c0: b * 6144 + c1]

        def qT(h):
            return rblk(h)

        def vst(b, c0, c1):
            return RA[:, 16 * 512 + b * 2048 + c0: 16 * 512 + b * 2048 + c1]

        def ktb(s):
            return rblk(32 + s)

        def vsb(s):
            return rblk(36 + s)

        def ptb(s):
            return rblk(40 + s)

        def ksg(s):
            return rblk(44 + s)

        def Rk(i):
            return ("R", i)

        def build(P):
            emit = P.emit

            def issue_unit(i):
                src, tile_, j = P.unit_specs[i]
                slot = i % NW
                ctile = 1 + j % 2
                if tile_ > ctile:
                    emit("pool", lambda e, s=slot, j=j: e.dma_start(out=WB[:, s].rearrange("p k c -> p (k c)"), in_=wcache_at(j)),
                         reads=[("wc", j)], writes=[("w", slot)], dma_sem="w%d" % slot)
                    return
                emit("pool", lambda e, s=slot, a=src: e.dma_start(out=WB[:, s], in_=a),
                     reads=(), writes=[("w", slot)], dma_sem="w%d" % slot)
                if tile_ == ctile and tile_ < nt - 1:
                    emit("sp", lambda e, s=slot, j=j: e.dma_start(out=wcache_at(j), in_=WB[:, s].rearrange("p k c -> p (k c)")),
                         reads=[("w", slot)], writes=[("wc", j)], dma_sem="wst%d" % slot)

            def unit(src2d, r0, c0):
                src = src2d[r0:r0 + 2048, c0:c0 + 512].rearrange("(j p) c -> p j c", p=128)
                if (not P.in_mod) and P.cur_tile == 0 and P.mod_next < len(modsrc):
                    P.ucnt += 1
                    if P.ucnt % 3 == 0:
                        i_ = P.mod_next
                        P.mod_next += 1
                        mod_unit(i_)
                if P.collect:
                    if P.in_mod:
                        P.unit_specs.append((src, -1, 0))
                        return 0
                    tl = P.cur_tile
                    j = P.tile_units.get(tl, 0)
                    P.tile_units[tl] = j + 1
                    P.unit_specs.append((src, tl, j))
                    return 0
                i = P.unit_next
                P.unit_next += 1
                lim = min(len(P.unit_specs), i + NW)
                while P.unit_issued < lim:
                    issue_unit(P.unit_issued)
                    P.unit_issued += 1
                return i % NW

            def mm(out, lhsT, rhs, start, stop, reads, writes):
                emit("pe", lambda e: e.matmul(out, lhsT, rhs, start=start, stop=stop, skip_group_check=True),
                     reads=reads, writes=writes)

            def tr(out, in_, ident, reads, writes):
                emit("pe", lambda e: e.transpose(out, in_, ident), reads=reads, writes=writes)

            def act(out, in_, func, reads, writes, bias=None, scale=None):
                kw = {}
                if bias is not None:
                    kw["bias"] = bias
                if scale is not None:
                    kw["scale"] = scale
                emit("act", lambda e: e.activation(out=out, in_=in_, func=func, **kw), reads=reads, writes=writes)

            def stt(out, in0, scalar, in1, op0, op1, reads, writes):
                emit("dve", lambda e: e.scalar_tensor_tensor(out=out, in0=in0, scalar=scalar, in1=in1, op0=op0, op1=op1),
                     reads=reads, writes=writes)

            def tt_(out, in0, in1, op, reads, writes, eng="dve"):
                emit(eng, lambda e: e.tensor_tensor(out=out, in0=in0, in1=in1, op=op), reads=reads, writes=writes)

            def ts(out, in0, s1, s2, op0, op1, reads, writes, eng="dve"):
                emit(eng, lambda e: e.tensor_scalar(out=out, in0=in0, scalar1=s1, scalar2=s2, op0=op0, op1=op1),
                     reads=reads, writes=writes)

            def cp(out, in_, reads, writes, eng="dve"):
                emit(eng, lambda e: e.tensor_copy(out=out, in_=in_), reads=reads, writes=writes)

            def recip(out, in_, reads, writes):
                emit("dve", lambda e: e.reciprocal(out=out, in_=in_), reads=reads, writes=writes)

            def dma(eng, out, in_, reads, writes, sem):
                emit(eng, lambda e: e.dma_start(out=out, in_=in_), reads=reads, writes=writes, dma_sem=sem)

            emit("pool", lambda e: e.memset(IDT[:], 0.0), writes=[("idt",)])
            emit("pool", lambda e: e.affine_select(out=IDT[:], in_=IDT[:], pattern=[[-1, 128]], compare_op=ALU.not_equal,
                                                   fill=1.0, base=0, channel_multiplier=1),
                 reads=[("idt",)], writes=[("idt",)])
            emit("pool", lambda e: e.memset(TRI[:], 1.0), writes=[("tri",)])
            emit("pool", lambda e: e.memset(ONESB[:], 1.0), writes=[("onesb",)])
            emit("pool", lambda e: e.memset(EPSC[:], EPS), writes=[("epsc",)])
            emit("pool", lambda e: e.memset(ONEC[:], 1.0), writes=[("onec",)])
            emit("pool", lambda e: e.affine_select(out=TRI[:], in_=TRI[:], pattern=[[1, 128]], compare_op=ALU.is_ge,
                                                   fill=0.0, base=0, channel_multiplier=-1),
                 reads=[("tri",)], writes=[("tri",)])
            emit("pool", lambda e: e.memset(SEL127[:], 0.0), writes=[("sel127",)])
            emit("pool", lambda e: e.affine_select(out=SEL127[:], in_=SEL127[:], pattern=[[0, 128]], compare_op=ALU.not_equal,
                                                   fill=1.0, base=-127, channel_multiplier=1),
                 reads=[("sel127",)], writes=[("sel127",)])
            emit("pool", lambda e: e.memset(SEL0[:], 0.0), writes=[("sel0",)])
            emit("pool", lambda e: e.affine_select(out=SEL0[:], in_=SEL0[:], pattern=[[0, 128]], compare_op=ALU.not_equal,
                                                   fill=1.0, base=0, channel_multiplier=1),
                 reads=[("sel0",)], writes=[("sel0",)])
            ts(NEGM[:], TRI[:], -1.0, 30000.0, ALU.add, ALU.mult, [("tri",)], [("negm",)], eng="pool")
            cp(IDB[:], IDT[:], [("idt",)], [("idb",)], eng="pool")
            emit("pool", lambda e: e.memset(FAC[:], 1.0), writes=[("fac",)])
            emit("pool", lambda e: e.memset(FACD[:], 0.0), writes=[("facd", j) for j in range(4)])

            dma("sp", BFB[:], b_f.partition_broadcast(128), (), [("bfb",)], "par")
            dma("pool", WF[:], w_f.rearrange("(kc p) h -> p kc h", p=128), (), [("wf",)], "par")

            def rows_T(src, nrows, dst, dkey, func=None):
                s = P.rot("stg", 2)
                dma("sp", STG[0:nrows, s, :], src, (), [("stg", s)], "stg%d" % s)
                b = P.rbank()
                tr(PS[:, b, 0:nrows], STG[0:nrows, s, :], IDT[0:nrows, 0:nrows], [("stg", s), ("idt",)], [("ps", b)])
                if func is None:
                    cp(dst, PS[:, b, 0:nrows], [("ps", b)], [dkey])
                else:
                    act(dst, PS[:, b, 0:nrows], func, [("ps", b)], [dkey])

            rows_T(ada_b[0:128, :], 128, BIAST[:, 0:128], ("biast", 0))
            rows_T(ada_b[128:256, :], 128, BIAST[:, 128:256], ("biast", 1))
            rows_T(ada_b[256:384, :], 128, BIAST[:, 256:384], ("biast", 2))
            rows_T(kv_ada_b[:, :], 32, BIAST[:, 384:416], ("biast", 3))
            rows_T(norm_g[:, :], 128, NG[:, :], ("ng",))
            rows_T(kv_norm_g[:, :], 16, MISC[:, 0:16], ("misc", 0))
            rows_T(k_norm_g[:, :], 1, MISC[:, 16:17], ("misc", 1))
            rows_T(q_norm_g[:, :], 2, MISC[:, 17:19], ("misc", 2))
            for a in range(2):
                rows_T(g_ln_g[a], 48, LNP[:, a, 0:48], ("lnp", a, 0))
                rows_T(g_ln_b[a], 48, LNP[:, a, 48:96], ("lnp", a, 1))
            rows_T(c_d[:, :], 16, SCT[:, :], ("sct",), func=AF.Silu)

            for a in range(2):
                for g in range(8):
                    s = P.rot("stg", 2)
                    dma("sp", STG[:, s, :], g_ws[a, g], (), [("stg", s)], "stg%d" % s)
                    b = P.rbank()
                    tr(PS[:, b, 0:128], STG[:, s, :], IDT[:], [("stg", s), ("idt",)], [("ps", b)])
                    cp(WST[:, a, g, :], PS[:, b, 0:128], [("ps", b)], [("wst", a, g)])
                    emit("dve", lambda e, a=a, g=g: e.memset(WST[64:128, a, g, 0:64], 0.0),
                         reads=[("wst", a, g)], writes=[("wst", a, g)])
                    b2 = P.rbank()
                    mm(PS[:, b2, 0:128], ONESB[:], WST[:, a, g, :], True, True, [("onesb",), ("wst", a, g)], [("ps", b2)])
                    cp(RSB[:, a, g, :], PS[:, b2, 0:128], [("ps", b2)], [("rsb", a, g)])

            P.tag = "mod"
            modsrc = []
            for l in range(2):
                for u in range(24):
                    modsrc.append((ada_w[l], u * 512, l * 96 + u * 4))
            for u in range(8):
                modsrc.append((kv_ada_w, u * 512, 384 + u * 4))
            for l in range(2, 4):
                for u in range(24):
                    modsrc.append((ada_w[l], u * 512, l * 96 + u * 4))

            def acol_layer(l):
                for sub in range(2):
                    sc0 = l * 96 + (1 + 3 * sub) * 16
                    stt(ACOL[:, l * 2 + sub, :], MOD[:, sc0:sc0 + 16], 1.0, NG[:, l * 32 + sub * 16: l * 32 + sub * 16 + 16],
                        ALU.add, ALU.mult, [("modc", sc0 // 4 + i) for i in range(4)] + [("ng",)], [("acol", l * 2 + sub)])

            def mod_unit(idx):
                src2d, c0, col0 = modsrc[idx]
                tag_save = P.tag
                P.tag = "mod"
                P.in_mod = True
                slot = unit(src2d, 0, c0)
                P.in_mod = False
                b = P.rbank()
                for m in range(4):
                    for kc in range(KC):
                        mm(PS[:, b, m:m + 1], WB[:, slot, kc, m * 128:(m + 1) * 128], SCT[:, kc:kc + 1],
                           kc == 0, kc == KC - 1, [("w", slot), ("sct",)], [("ps", b)])
                tt_(MOD[:, col0:col0 + 4], PS[:, b, 0:4], BIAST[:, col0:col0 + 4], ALU.add,
                    [("ps", b)] + [("biast", i) for i in range(4)], [("modc", col0 // 4)])
                if idx == 23:
                    acol_layer(0)
                elif idx == 47:
                    acol_layer(1)
                elif idx == 55:
                    stt(ACOL[:, 8, :], MOD[:, 400:416], 1.0, MISC[:, 0:16], ALU.add, ALU.mult,
                        [("modc", 100 + i) for i in range(4)] + [("misc", 0)], [("acol", 8)])
                elif idx == 79:
                    acol_layer(2)
                elif idx == 103:
                    acol_layer(3)
                P.tag = tag_save

            def mod_flush(upto):
                while P.mod_next < upto:
                    i_ = P.mod_next
                    P.mod_next += 1
                    mod_unit(i_)

            mod_flush(24)

            def modk(l, m, kc):
                base = 384 + m * 16 if l == 4 else l * 96 + m * 16
                return ("modc", (base + kc) // 4)

            def modcol(l, m, kc):
                base = 384 + m * 16 if l == 4 else l * 96 + m * 16
                return MOD[:, base + kc: base + kc + 1]

            def stats_begin():
                P.ssb = P.rbank()
                P.ssn = 0

            def stats_sq(kc, eng):
                s = P.rot("sq", 5)
                if eng == "act":
                    act(SQ[:, s, :], XT[:, kc, :], AF.Square, [("x", kc)], [("sq", s)])
                else:
                    tt_(SQ[:, s, :], XT[:, kc, :], XT[:, kc, :], ALU.mult, [("x", kc)], [("sq", s)], eng=eng)

                def acc():
                    n = P.ssn
                    P.ssn = n + 1
                    mm(PS[:, P.ssb, :], ONESB[:], SQ[:, s, :], n == 0, n == KC - 1, [("sq", s), ("onesb",)], [("ps", P.ssb)])
                return acc

            def stats_end():
                P.tag = "norm"
                t = P.tmpi()
                act(TMP[:, t, :], PS[:, P.ssb, :], AF.Ln, [("ps", P.ssb), ("epsc",)], [("tmp", t)], bias=EPSC[:], scale=1.0 / D)
                act(RSTD[:], TMP[:, t, :], AF.Exp, [("tmp", t)], [("rstd",)], scale=-0.5)

            def norm_mod(aidx, l, mshift):
                P.tag = "norm"
                for kc in range(KC):
                    t2 = P.tmpi()
                    tt_(TMP[:, t2, :], XT[:, kc, :], RSTD[:], ALU.mult, [("x", kc), ("rstd",)], [("tmp", t2)])
                    if kc % 3 != 2:
                        act(HT[:, kc, :], TMP[:, t2, :], AF.Identity, [("tmp", t2), ("acol", aidx), modk(l, mshift, kc)], [("h", kc)],
                            bias=modcol(l, mshift, kc), scale=ACOL[:, aidx, kc:kc + 1])
                    else:
                        ts(HT[:, kc, :], TMP[:, t2, :], ACOL[:, aidx, kc:kc + 1], modcol(l, mshift, kc), ALU.mult, ALU.add,
                           [("tmp", t2), ("acol", aidx), modk(l, mshift, kc)], [("h", kc)], eng="pool")

            def gemm2(src2d, r0, l, mgate, final=False):
                P.tag = "gemm2"
                if final:
                    stats_begin()
                deferred = []
                for dg in range(4):
                    slot = unit(src2d, r0, dg * 512)
                    order = [(j, dc) for j in range(2) for dc in range(3)] + [(0, 3), (1, 3)] + \
                            [(j, dc) for j in range(2, 12) for dc in range(4)]
                    for (j, dc) in order:
                        mm(PS[:, 4 + dc, :], WB[:, slot, j, dc * 128:(dc + 1) * 128], HID[:, j, :],
                           j == 0, False, [("w", slot), ("hid", j)], [("ps", 4 + dc)])
                    for f in deferred:
                        f()
                    deferred = []
                    for dc in range(4):
                        for j in range(12, 16):
                            mm(PS[:, 4 + dc, :], WB[:, slot, j, dc * 128:(dc + 1) * 128], HID[:, j, :],
                               False, j == 15, [("w", slot), ("hid", j)], [("ps", 4 + dc)])
                        kc = dg * 4 + dc
                        stt(XT[:, kc, :], PS[:, 4 + dc, :], modcol(l, mgate, kc), XT[:, kc, :], ALU.mult, ALU.add,
                            [("ps", 4 + dc), modk(l, mgate, kc), ("x", kc)], [("x", kc)])
                        if final:
                            deferred.append(stats_sq(kc, "act"))
                for f in deferred:
                    f()
                if final:
                    stats_end()

            def gemm1_fm(src2d, c0, evac):
                slot = unit(src2d, 0, c0)
                pending = None
                for m in range(4):
                    b = P.rbank()
                    for kc in range(KC):
                        mm(PS[:, b, :], WB[:, slot, kc, m * 128:(m + 1) * 128], HT[:, kc, :],
                           kc == 0, kc == KC - 1, [("w", slot), ("h", kc)], [("ps", b)])
                    if pending is not None:
                        pending()
                    pending = evac(m, b)
                if pending is not None:
                    pending()

            def gemm1_tm(src2d, c0, evac):
                slot = unit(src2d, 0, c0)
                for blk in range(4):
                    b = P.rbank()
                    for kc in range(KC):
                        mm(PS[:, b, :], HT[:, kc, blk * 128:(blk + 1) * 128], WB[:, slot, kc, :],
                           kc == 0, kc == KC - 1, [("w", slot), ("h", kc)], [("ps", b)])
                    evac(blk, b)

            def head_norm(b, gcol, gkey, out_ap, out_keys, after=None):
                s = P.rot("sq", 5)
                act(SQ[:, s, :], PS[:, b, :], AF.Square, [("ps", b)], [("sq", s)])

                def tail():
                    b2 = 4 + P.rot("hnb", 4)
                    mm(PS[:, b2, :], ONESB[:], SQ[:, s, :], True, True, [("sq", s), ("onesb",)], [("ps", b2)])
                    t = P.tmpi()
                    act(TMP[:, t, :], PS[:, b2, :], AF.Ln, [("ps", b2), ("epsc",)], [("tmp", t)], bias=EPSC[:], scale=1.0 / 128.0)
                    t2 = P.tmpi()
                    act(TMP[:, t2, :], TMP[:, t, :], AF.Exp, [("tmp", t)], [("tmp", t2)], scale=-0.5)
                    stt(out_ap, PS[:, b, :], gcol, TMP[:, t2, :], ALU.mult, ALU.mult,
                        [("ps", b), ("tmp", t2), gkey], out_keys)
                    if after is not None:
                        after()
                return tail

            def mlp(l):
                for q in range(4):
                    P.tag = "mlp1"
                    def ev(m, b, q=q):
                        t = P.tmpi()
                        act(TMP[:, t, :], PS[:, b, :], AF.Relu, [("ps", b)], [("tmp", t)])
                        stt(HID[:, m_base[0] + m, :], PS[:, b, :], 0.0, TMP[:, t, :], ALU.max, ALU.mult,
                            [("ps", b), ("tmp", t)], [("hid", m_base[0] + m)])
                    m_base = [0]
                    for u in range(4):
                        m_base[0] = u * 4
                        gemm1_fm(mlp_w1[l], (q * 4 + u) * 512, ev)
                    gemm2(mlp_w2[l], q * 2048, l, 5, final=(q == 3 and l < nlayers - 1))

            def gmlp(l):
                a = l
                dma("sp", BSB[:].rearrange("p g i -> p (g i)"), g_bs[a * 1024:(a + 1) * 1024].partition_broadcast(128),
                    (), [("bsb",)], "bsb")
                P.tag = "gmlp_v"
                for n in range(12):
                    def ev(blk, b, n=n):
                        act(vtm(blk, n * 512, (n + 1) * 512), PS[:, b, :], AF.Gelu, [("ps", b)], [Rk(blk * 12 + n)])
                        emit("dve", lambda e, blk=blk, n=n: e.bn_stats(out=STATS[:, blk, n * 6:(n + 1) * 6],
                                                                        in_=vtm(blk, n * 512, (n + 1) * 512)),
                             reads=[Rk(blk * 12 + n)], writes=[("stats", blk, n)])
                    gemm1_tm(g_w_in[a], 6144 + n * 512, ev)
                P.tag = "gmlp_ln"
                for blk in range(4):
                    emit("dve", lambda e, blk=blk: e.bn_aggr(out=MV[:, blk, :], in_=STATS[:, blk, :]),
                         reads=[("stats", blk, n) for n in range(12)], writes=[("mv", blk)])
                    act(LNS[:, blk, 0:1], MV[:, blk, 1:2], AF.Sqrt, [("mv", blk), ("epsc",)], [("lns", blk, 0)],
                        bias=EPSC[:], scale=1.0)
                    recip(LNS[:, blk, 1:2], LNS[:, blk, 0:1], [("lns", blk, 0)], [("lns", blk, 1)])
                    stt(LNS[:, blk, 2:3], MV[:, blk, 0:1], -1.0, LNS[:, blk, 1:2], ALU.mult, ALU.mult,
                        [("mv", blk), ("lns", blk, 1)], [("lns", blk, 2)])
                    for n in range(12):
                        ts(vtm(blk, n * 512, (n + 1) * 512), vtm(blk, n * 512, (n + 1) * 512),
                           LNS[:, blk, 1:2], LNS[:, blk, 2:3], ALU.mult, ALU.add,
                           [Rk(blk * 12 + n), ("lns", blk, 1), ("lns", blk, 2)], [Rk(blk * 12 + n)])
                for third in range(3):
                    P.tag = "gmlp_u"
                    for u in range(4):
                        def ev(m, b, third=third, u=u):
                            ccl = u * 4 + m
                            cc = third * 16 + ccl
                            g = cc // 6
                            tu = P.tmpi()
                            act(TMP[:, tu, :], PS[:, b, :], AF.Gelu, [("ps", b)], [("tmp", tu)])
                            sb_ = P.rbank()
                            for blk in range(4):
                                mm(PS[:, sb_, blk * 128:(blk + 1) * 128], vtm(blk, cc * 128, (cc + 1) * 128), WST[:, a, g, :],
                                   True, True, [Rk(blk * 12 + cc // 4), ("wst", a, g)], [("ps", sb_)])
                            bs_ = P.rot("bt", 2)
                            stt(BT[:, bs_, :], RSB[:, a, g, :], LNP[:, a, 48 + cc: 48 + cc + 1], BSB[:, g, :],
                                ALU.mult, ALU.add, [("rsb", a, g), ("lnp", a, 1), ("bsb",)], [("bt", bs_)])
                            tsv = P.tmpi()
                            for blk in range(4):
                                stt(TMP[:, tsv, blk * 128:(blk + 1) * 128], PS[:, sb_, blk * 128:(blk + 1) * 128],
                                    LNP[:, a, cc:cc + 1], BT[:, bs_, :], ALU.mult, ALU.add,
                                    [("ps", sb_), ("lnp", a, 0), ("bt", bs_)],
                                    [("tmp", tsv, blk)] + ([("tmp", tsv)] if blk == 0 else []))
                            tt_(HID[:, ccl, :], TMP[:, tsv, :], TMP[:, tu, :], ALU.mult,
                                [("tmp", tsv)] + [("tmp", tsv, blk) for blk in range(4)] + [("tmp", tu)], [("hid", ccl)])
                        gemm1_fm(g_w_in[a], (third * 4 + u) * 512, ev)
                    gemm2(g_w_out[a], third * 2048, l, 2, final=(third == 2))

            def kv_phase(tti):
                norm_mod(8, 4, 0)
                P.tag = "kv"
                for u in range(4):
                    def ev(m, b, u=u):
                        hd = u * 4 + m
                        s = P.rot("ksg", 2)
                        return head_norm(b, MISC[:, 16:17], ("misc", 1), ksg(s), [Rk(44 + s)],
                                         after=lambda: dma("sp", kt_scr[hd, tti], ksg(s), [Rk(44 + s)], [("ktd", hd, tti)],
                                                           "ksg%d" % s))
                    gemm1_fm(w_kv, u * 512, ev)
                for u in range(4):
                    def ev(blk, b, u=u):
                        if (u + blk) % 2 == 0:
                            cp(vst(blk, u * 512, (u + 1) * 512), PS[:, b, :], [("ps", b)], [Rk(16 + blk * 4 + u)])
                        else:
                            act(vst(blk, u * 512, (u + 1) * 512), PS[:, b, :], AF.Identity, [("ps", b)], [Rk(16 + blk * 4 + u)])
                    gemm1_tm(w_kv, 2048 + u * 512, ev)
                for blk in range(4):
                    dma("sp", v_scr[tti, :, :, blk, :].rearrange("h p d -> p h d"),
                        vst(blk, 0, 2048).rearrange("p (h d) -> p h d", h=H),
                        [Rk(16 + blk * 4 + u) for u in range(4)], [("vd", tti, blk)], "vst%d" % blk)
                for blk in range(4):
                    gb = tti * 4 + blk
                    fb = P.rbank()
                    for kc in range(KC):
                        mm(PS[:, fb, 0:H], HT[:, kc, blk * 128:(blk + 1) * 128], WF[:, kc, :],
                           kc == 0, kc == KC - 1, [("h", kc), ("wf",)], [("ps", fb)])
                    tt_(ZS[:, 0, :], PS[:, fb, 0:H], BFB[:], ALU.add, [("ps", fb), ("bfb",)], [("zs", 0)])
                    act(ZS[:, 1, :], ZS[:, 0, :], AF.Exp, [("zs", 0)], [("zs", 1)], scale=-1.0)
                    act(ZS[:, 2, :], ZS[:, 1, :], AF.Ln, [("zs", 1), ("onec",)], [("zs", 2)], bias=ONEC[:], scale=1.0)
                    cb_ = P.rbank()
                    mm(PS[:, cb_, 0:H], TRI[:], ZS[:, 2, :], True, gb == 0, [("tri",), ("zs", 2)], [("ps", cb_)])
                    if gb > 0:
                        mm(PS[:, cb_, 0:H], SEL127[:], NF[:, gb - 1, :], False, True,
                           [("sel127",), ("nf", gb - 1)], [("ps", cb_)])
                    cp(NF[:, gb, :], PS[:, cb_, 0:H], [("ps", cb_)], [("nf", gb)])
                    c2 = P.rbank()
                    mm(PS[:, c2, 0:H], SEL0[:], NF[:, gb, :], True, True, [("sel0",), ("nf", gb)], [("ps", c2)])
                    cp(CB[:, gb, :], PS[:, c2, 0:H], [("ps", c2)], [("cb", gb)])

            def attention(l, tti):
                bl = l - 2
                P.tag = "attn_q"
                for u in range(4):
                    def ev(m, b, u=u):
                        hd = u * 4 + m
                        return head_norm(b, MISC[:, 17 + bl:18 + bl], ("misc", 2), qT(hd), [Rk(hd)])
                    gemm1_fm(attn_wq[bl], u * 512, ev)
                P.tag = "attn"
                if tti > 0:
                    for j in range(1, 4):
                        tt_(FACD[:, j, :], CB[:, tti * 4 + j, :], CB[:, tti * 4, :], ALU.subtract,
                            [("cb", tti * 4 + j), ("cb", tti * 4)], [("facd", j)])
                    act(FAC[:, 1:4, :], FACD[:, 1:4, :], AF.Exp, [("facd", j) for j in range(1, 4)], [("fac",)], scale=-1.0)
                groups = [(hd, st) for hd in range(H) for st in range(tti + 1)]
                steps = [(g, i) for g in range(len(groups)) for i in range(4)]
                LA = 3
                gslots = {}
                pend = {}
                hstate = {}

                def load_group(g):
                    hd, st = groups[g]
                    ks = P.rot("kt", 4)
                    vs = P.rot("vs", 4)
                    dma("sp", ktb(ks), kt_scr[hd, st], [("ktd", hd, st)], [Rk(32 + ks)], "kt%d" % ks)
                    dma("sp", vsb(vs).rearrange("p (b d) -> p b d", b=4), v_scr[st, hd],
                        [("vd", st, blk) for blk in range(4)], [Rk(36 + vs)], "vs%d" % vs)
                    gslots[g] = (ks, vs)

                def s_stage(k):
                    g, i = steps[k]
                    hd, st = groups[g]
                    if i == 0 and g + 2 < len(groups):
                        load_group(g + 2)
                    ks, vs = gslots[g]
                    gsb = st * 4 + i
                    diag = (st == tti)
                    j0 = i if diag else 0
                    c0 = j0 * 128
                    sbk = P.rbank()
                    mm(PS[:, sbk, c0:512], ktb(ks)[:, i * 128:(i + 1) * 128], qT(hd)[:, c0:512], True, not diag,
                       [Rk(32 + ks), Rk(hd)], [("ps", sbk)])
                    if diag:
                        mm(PS[:, sbk, c0:c0 + 128], IDB[:], NEGM[:], False, True, [("idb",), ("negm",)], [("ps", sbk)])
                    pt = P.rot("pt", 4)
                    b4 = P.rot("b4", 8)
                    if diag:
                        ts(B4[:, b4, :], CB[:, tti * 4:tti * 4 + 4, hd], -1.0, NF[:, gsb, hd:hd + 1], ALU.mult, ALU.add,
                           [("cb", tti * 4 + j) for j in range(4)] + [("nf", gsb)], [("b4", b4)], eng="pool")
                        for j in range(j0, 4):
                            act(ptb(pt)[:, j * 128:(j + 1) * 128], PS[:, sbk, j * 128:(j + 1) * 128], AF.Exp,
                                [("ps", sbk), ("b4", b4)], [("pt", pt, j)] + ([Rk(40 + pt)] if j == j0 else []),
                                bias=B4[:, b4, j:j + 1], scale=INV_SQRT)
                    else:
                        ts(B4[:, b4, 0:1], NF[:, gsb, hd:hd + 1], CB[:, tti * 4, hd:hd + 1], None, ALU.subtract, ALU.bypass,
                           [("cb", tti * 4), ("nf", gsb)], [("b4", b4)], eng="pool")
                        act(ptb(pt)[:, :], PS[:, sbk, :], AF.Exp, [("ps", sbk), ("b4", b4)],
                            [("pt", pt, j) for j in range(4)] + [Rk(40 + pt)], bias=B4[:, b4, 0:1], scale=INV_SQRT)
                    pend[k] = (hd, st, i, vs, pt, c0, j0)

                def pv_stage(k):
                    hd, st, i, vs, pt, c0, j0 = pend.pop(k)
                    diag = (st == tti)
                    po = 6 if diag else 4
                    if tti == 0:
                        po = 4 + 2 * (hd % 2)
                    pd = po + 1
                    first = (i == 0) if diag else (st == 0 and i == 0)
                    last = (i == 3) if diag else (st == tti - 1 and i == 3)
                    rd = [("pt", pt, j) for j in range(j0, 4)] + [Rk(40 + pt)]
                    mm(PS[:, po, c0:512], vsb(vs)[:, i * 128:(i + 1) * 128], ptb(pt)[:, c0:512], first, last,
                       rd + [Rk(36 + vs)], [("ps", po)])
                    mm(PS[:, pd, c0:512], ONESB[:], ptb(pt)[:, c0:512], first, last,
                       rd + [("onesb",)], [("ps", pd)])
                    if last and not diag:
                        to = P.tmpi()
                        td = P.tmpi()
                        for j in range(4):
                            ts(TMP[:, to, j * 128:(j + 1) * 128], PS[:, 4, j * 128:(j + 1) * 128], FAC[:, j, hd:hd + 1], None,
                               ALU.mult, ALU.bypass, [("ps", 4), ("fac",)], [("tmp", to, j)] + ([("tmp", to)] if j == 0 else []))
                            ts(TMP[:, td, j * 128:(j + 1) * 128], PS[:, 5, j * 128:(j + 1) * 128], FAC[:, j, hd:hd + 1], None,
                               ALU.mult, ALU.bypass, [("ps", 5), ("fac",)], [("tmp", td, j)] + ([("tmp", td)] if j == 0 else []))
                        hstate[hd] = (to, td)
                    if last and diag:
                        if tti == 0:
                            td = P.tmpi()
                            recip(TMP[:, td, :], PS[:, pd, :], [("ps", pd)], [("tmp", td)])
                            tt_(HID[:, hd, :], PS[:, po, :], TMP[:, td, :], ALU.mult, [("ps", po), ("tmp", td)], [("hid", hd)])
                        else:
                            to, td = hstate.pop(hd)
                            allk_o = [("tmp", to)] + [("tmp", to, j) for j in range(4)]
                            allk_d = [("tmp", td)] + [("tmp", td, j) for j in range(4)]
                            tt_(TMP[:, td, :], PS[:, 7, :], TMP[:, td, :], ALU.add, [("ps", 7)] + allk_d, [("tmp", td)])
                            recip(TMP[:, td, :], TMP[:, td, :], [("tmp", td)], [("tmp", td)])
                            tt_(TMP[:, to, :], PS[:, 6, :], TMP[:, to, :], ALU.add, [("ps", 6)] + allk_o, [("tmp", to)])
                            tt_(HID[:, hd, :], TMP[:, to, :], TMP[:, td, :], ALU.mult, [("tmp", to), ("tmp", td)], [("hid", hd)])

                for g in range(min(2, len(groups))):
                    load_group(g)
                for k in range(len(steps) + LA):
                    if k < len(steps):
                        s_stage(k)
                    if k - LA >= 0:
                        pv_stage(k - LA)
                gemm2(attn_wo[bl], 0, l, 2, final=True)

            for tti in range(nt):
                P.tag = "xin"
                P.cur_tile = tti
                dma("sp", RAf[:, :].rearrange("p (b d) -> p b d", b=4),
                    x_d[tti * T:(tti + 1) * T, :].rearrange("(b p) d -> p b d", p=128),
                    (), [Rk(i) for i in range(32)], "xin")
                for kc in range(KC):
                    b = P.rbank()
                    for blk in range(4):
                        tr(PS[:, b, blk * 128:(blk + 1) * 128], xst(blk, kc * 128, (kc + 1) * 128), IDT[:],
                           [Rk(blk * 8 + kc // 2), ("idt",)], [("ps", b)])
                    if kc % 2 == 0:
                        cp(XT[:, kc, :], PS[:, b, :], [("ps", b)], [("x", kc)])
                    else:
                        act(XT[:, kc, :], PS[:, b, :], AF.Identity, [("ps", b)], [("x", kc)])
                P.tag = "norm"
                stats_begin()
                for kc in range(KC):
                    stats_sq(kc, "pool" if kc % 4 == 1 else ("dve" if kc % 8 == 7 else "act"))()
                stats_end()
                for l in range(nlayers):
                    if tti == 0:
                        mod_flush([24, 48, 80, 104][l])
                    if l == 2:
                        kv_phase(tti)
                    norm_mod(l * 2, l, 0)
                    if l < 2:
                        gmlp(l)
                    else:
                        attention(l, tti)
                    norm_mod(l * 2 + 1, l, 3)
                    mlp(l)
                P.tag = "xout"
                for blk in range(4):
                    for q in range(4):
                        b = P.rbank()
                        for r in range(4):
                            kc = q * 4 + r
                            tr(PS[:, b, r * 128:(r + 1) * 128], XT[:, kc, blk * 128:(blk + 1) * 128], IDT[:],
                               [("x", kc), ("idt",)], [("ps", b)])
                        if q % 2 == 0:
                            cp(xst(blk, q * 512, (q + 1) * 512), PS[:, b, :], [("ps", b)], [Rk(blk * 8 + q * 2), Rk(blk * 8 + q * 2 + 1)])
                        else:
                            act(xst(blk, q * 512, (q + 1) * 512), PS[:, b, :], AF.Identity, [("ps", b)],
                                [Rk(blk * 8 + q * 2), Rk(blk * 8 + q * 2 + 1)])
                    dma("sp", out_d[tti * T + blk * 128: tti * T + (blk + 1) * 128, :], xst(blk, 0, 2048),
                        [Rk(blk * 8 + i) for i in range(8)], [("outd", tti, blk)], "xo%d" % blk)

        P1 = Prog(collect=True)
        build(P1)
        P2 = Prog(collect=False, unit_specs=P1.unit_specs)
        build(P2)
        P2.finalize()
        P = P2
        nc._prog = P

        sem_names = list(ENGS[:4]) + sorted(P.dmacnt.keys())
        sems = {}
        for nm in sem_names:
            sems[nm] = es.enter_context(nc.semaphore("s_" + nm))
        block = es.enter_context(nc.Block())

        def replay(engname, e):
            for op in P.ops[engname]:
                for d in op.waits:
                    e.wait_ge(sems[d.sem], d.val)
                ins = op.fn(e)
                if op.isdma:
                    ins.then_inc(sems[op.sem], 16)
                elif op.flag:
                    ins.then_inc(sems[op.sem], 1)
            if engname == "sp":
                for nm, c in P.dmacnt.items():
                    if nm.startswith("xo"):
                        e.wait_ge(sems[nm], 16 * c)

        @block.tensor
        def _(e):
            replay("pe", e)

        @block.scalar
        def _(e):
            replay("act", e)

        @block.vector
        def _(e):
            replay("dve", e)

        @block.gpsimd
        def _(e):
            replay("pool", e)

        @block.sync
        def _(e):
            replay("sp", e)

    return nc


_CACHE = {}


def make_in_maps(inputs, nt):
    f = lambda a: np.ascontiguousarray(np.asarray(a, dtype=np.float32))
    shared = {
        "ada_w": f(inputs["ada_w"]),
        "ada_b": f(inputs["ada_b"]).reshape(384, 128),
        "norm_g": f(inputs["norm_g"]).reshape(128, 128),
        "mlp_w1": f(inputs["mlp_w1"]),
        "mlp_w2": f(inputs["mlp_w2"]),
        "gmlp_w_in": f(inputs["gmlp_w_in"]),
        "gmlp_ln_g": f(inputs["gmlp_ln_g"]).reshape(2, 48, 128),
        "gmlp_ln_b": f(inputs["gmlp_ln_b"]).reshape(2, 48, 128),
        "gmlp_ws": f(inputs["gmlp_ws"]),
        "gmlp_bs": f(inputs["gmlp_bs"]).reshape(2 * 8 * 128),
        "gmlp_w_out": f(inputs["gmlp_w_out"]),
        "kv_norm_g": f(inputs["kv_norm_g"]).reshape(16, 128),
        "kv_ada_w": f(inputs["kv_ada_w"]),
        "kv_ada_b": f(inputs["kv_ada_b"]).reshape(32, 128),
        "w_kv": f(inputs["w_kv"]),
        "k_norm_g": f(inputs["k_norm_g"]).reshape(1, 128),
        "w_f": f(inputs["w_f"]),
        "b_f": f(inputs["b_f"]).reshape(16),
        "attn_wq": f(inputs["attn_wq"]),
        "q_norm_g": f(inputs["q_norm_g"]).reshape(2, 128),
        "attn_wo": f(inputs["attn_wo"]),
    }
    x = f(inputs["x"])
    c = f(inputs["c"])
    maps = []
    for b in range(8):
        m = dict(shared)
        m["x"] = x[b]
        m["c"] = c[b].reshape(16, 128)
        maps.append(m)
    return maps


def kernel(**inputs):
    nt = CFG["NT"]
    nl = CFG["LAYERS"]
    key = (nt, nl)
    if key not in _CACHE:
        _CACHE[key] = build_program(nt, nl)
    nc = _CACHE[key]
    maps = make_in_maps(inputs, nt)
    res = run_bass_kernel_spmd(nc, maps, core_ids=list(range(8)))
    out = np.stack([np.asarray(r["out"], dtype=np.float32) for r in res.results], axis=0)
    return out
```
